# Optimizing a Trainium2 kernel written in Bass

```python
import math
import jax, jax.numpy as jnp
from jax import lax
import numpy as np

D_MODEL = 1024
BATCH = 16
SEQ = 4096
DEPTH = 2
DEC_BATCH = 32
DEC_SEQ = 2048
PAST_LEN = 128

ATT_HEADS = 4
ATT_QK_DIM = 64
ATT_V_DIM = 128
ATT_WIDTH = ATT_HEADS * ATT_V_DIM
LSTM_HEADS = 4
LSTM_QK_DIM = 64
LSTM_V_DIM = 128
LSTM_WIDTH = LSTM_HEADS * LSTM_V_DIM
MIX_WIDTH = ATT_WIDTH + LSTM_WIDTH
N_GATES = 4 * LSTM_HEADS
COL_WIDTHS = (ATT_HEADS * 2 * ATT_QK_DIM,
              ATT_HEADS * 2 * ATT_QK_DIM,
              ATT_WIDTH,
              LSTM_HEADS * LSTM_QK_DIM,
              LSTM_HEADS * LSTM_QK_DIM,
              LSTM_WIDTH,
              LSTM_WIDTH,
              N_GATES)
IN_WIDTH = 512 + 512 + 512 + 256 + 256 + 512 + 512 + 16
V_COLS = ((1024, 1536), (2048, 2560))
D_FF = 2816
CONV_WIDTH = 3
Q_BLOCK = 128
CHUNK = 128
ALPHA = (2 * DEPTH) ** 0.25
BETA = (8 * DEPTH) ** -0.25
EPS = 1e-5

kernel_name = "hybrid_diffattn_mlstm_encoder"


def _split_cols(proj):
    out, start = [], 0
    for w in COL_WIDTHS:
        out.append(proj[..., start:start + w])
        start += w
    return out


def _layer_norm(x, g, b):
    xf = x.astype(jnp.float32)
    mu = jnp.mean(xf, -1, keepdims=True)
    var = jnp.mean(jnp.square(xf - mu), -1, keepdims=True)
    y = (xf - mu) * lax.rsqrt(var + EPS) * g.astype(jnp.float32) + b.astype(jnp.float32)
    return y.astype(x.dtype)


def _head_rmsnorm(h, gain):
    B, S, H, dv = h.shape
    hf = h.astype(jnp.float32)
    hf = hf * lax.rsqrt(jnp.mean(hf * hf, -1, keepdims=True) + EPS)
    return hf.reshape(B, S, H * dv) * gain.astype(jnp.float32)


def _alibi_slopes(n):
    return jnp.power(2.0, -8.0 * jnp.arange(1, n + 1, dtype=jnp.float32) / n)


def _diff_attention(q, k, v, lam):
    B, S, H, _, dk = q.shape
    dv = v.shape[-1]
    nb = S // Q_BLOCK
    qb = (q * (dk ** -0.5)).reshape(B, nb, Q_BLOCK, H, 2, dk).transpose(1, 0, 2, 3, 4, 5)
    pos_k = jnp.arange(S, dtype=jnp.float32)
    slopes = _alibi_slopes(H)

    def block(args):
        qi, start = args
        pos_q = start + jnp.arange(Q_BLOCK, dtype=jnp.float32)
        bias = -slopes[:, None, None] * jnp.abs(pos_q[:, None] - pos_k[None, :])
        s = jnp.einsum('bqhcd,bshcd->bhcqs', qi, k, preferred_element_type=jnp.float32)
        p = jax.nn.softmax(s + bias[None, :, None], axis=-1)
        a = p[:, :, 0] - lam * p[:, :, 1]
        return jnp.einsum('bhqs,bshd->bqhd', a.astype(v.dtype), v)

    starts = jnp.arange(nb, dtype=jnp.float32) * Q_BLOCK
    out = lax.map(block, (qb, starts))
    return out.transpose(1, 0, 2, 3, 4).reshape(B, S, H, dv)


def _mlstm_direction(q, k, v, ig, lf):
    B, H, S, dk = q.shape
    dv = v.shape[-1]
    L = CHUNK
    nc = S // L
    qc = q.reshape(B, H, nc, L, dk)
    kc = k.reshape(B, H, nc, L, dk)
    vc = v.reshape(B, H, nc, L, dv)
    igc = ig.reshape(B, H, nc, L)
    b = jnp.cumsum(lf.reshape(B, H, nc, L), axis=-1)
    g = b[..., -1]
    tri = jnp.tril(jnp.ones((L, L), dtype=bool))
    D = jnp.where(tri, b[..., :, None] - b[..., None, :] + igc[..., None, :], -jnp.inf)
    w_end = g[..., None] - b + igc
    m_loc = jnp.max(w_end, -1)
    e_end = jnp.exp(w_end - m_loc[..., None])
    ke = kc.astype(jnp.float32) * e_end[..., None]
    C_loc = jnp.einsum('bhcld,bhcle->bhcde', ke, vc.astype(jnp.float32))
    n_loc = jnp.sum(ke, axis=3)

    def step(carry, inp):
        C, n, m = carry
        Cl, nl, ml, gc = inp
        m_new = jnp.maximum(gc + m, ml)
        a = jnp.exp(gc + m - m_new)
        bb = jnp.exp(ml - m_new)
        C_new = a[..., None, None] * C + bb[..., None, None] * Cl
        n_new = a[..., None] * n + bb[..., None] * nl
        return (C_new, n_new, m_new), (C, n, m)

    init = (jnp.zeros((B, H, dk, dv), jnp.float32), jnp.zeros((B, H, dk), jnp.float32),
            jnp.zeros((B, H), jnp.float32))
    xs = (jnp.moveaxis(C_loc, 2, 0), jnp.moveaxis(n_loc, 2, 0),
          jnp.moveaxis(m_loc, 2, 0), jnp.moveaxis(g, 2, 0))
    _, (C_prev, n_prev, m_prev) = lax.scan(step, init, xs)
    C_prev = jnp.moveaxis(C_prev, 0, 2)
    n_prev = jnp.moveaxis(n_prev, 0, 2)
    m_prev = jnp.moveaxis(m_prev, 0, 2)

    inter_log = b + m_prev[..., None]
    m_out = jnp.maximum(inter_log, jnp.max(D, -1))
    e_inter = jnp.exp(inter_log - m_out)
    P = jnp.exp(D - m_out[..., None])
    qf = qc.astype(jnp.float32)
    sqk = jnp.einsum('bhcjd,bhcsd->bhcjs', qf, kc.astype(jnp.float32)) * P
    num = (e_inter[..., None] * jnp.einsum('bhcjd,bhcde->bhcje', qf, C_prev)
           + jnp.einsum('bhcjs,bhcse->bhcje', sqk, vc.astype(jnp.float32)))
    den = e_inter * jnp.einsum('bhcjd,bhcd->bhcj', qf, n_prev) + jnp.sum(sqk, -1)
    h = num / jnp.maximum(jnp.abs(den), jnp.exp(-m_out))[..., None]
    return h.reshape(B, H, S, dv)


def _mixer(x, w_in, gate_bias, lam_q1, lam_k1, lam_q2, lam_k2, att_g, lstm_g, w_out, lam_init):
    B, S, _ = x.shape
    aq, ak, av, lq, lk, lv, lo, lg = _split_cols(x @ w_in)
    lam = (jnp.exp(jnp.sum(lam_q1.astype(jnp.float32) * lam_k1.astype(jnp.float32)))
           - jnp.exp(jnp.sum(lam_q2.astype(jnp.float32) * lam_k2.astype(jnp.float32))) + lam_init)
    att = _diff_attention(aq.reshape(B, S, ATT_HEADS, 2, ATT_QK_DIM),
                          ak.reshape(B, S, ATT_HEADS, 2, ATT_QK_DIM),
                          av.reshape(B, S, ATT_HEADS, ATT_V_DIM), lam)
    att = _head_rmsnorm(att, att_g) * (1.0 - lam_init)
    q = lq.reshape(B, S, LSTM_HEADS, LSTM_QK_DIM).transpose(0, 2, 1, 3)
    k = (lk.reshape(B, S, LSTM_HEADS, LSTM_QK_DIM) * (LSTM_QK_DIM ** -0.5)).transpose(0, 2, 1, 3)
    v = lv.reshape(B, S, LSTM_HEADS, LSTM_V_DIM).transpose(0, 2, 1, 3)
    gates = (lg.astype(jnp.float32) + gate_bias.astype(jnp.float32)).reshape(B, S, 4, LSTM_HEADS)
    gates = gates.transpose(2, 0, 3, 1)
    ig_f, lf_f = gates[0], jax.nn.log_sigmoid(gates[1])
    ig_b, lf_b = gates[2], jax.nn.log_sigmoid(gates[3])
    h_f = _mlstm_direction(q, k, v, ig_f, lf_f)
    fl = lambda t: jnp.flip(t, axis=2)
    h_b = fl(_mlstm_direction(fl(q), fl(k), fl(v), fl(ig_b), fl(lf_b)))
    h = (h_f + h_b).transpose(0, 2, 1, 3)
    lstm = jax.nn.sigmoid(lo.astype(jnp.float32)) * _head_rmsnorm(h, lstm_g)
    mixed = jnp.concatenate([att, lstm], axis=-1).astype(x.dtype)
    return mixed @ w_out


def _conv_ffn(x, w_gu, conv_w, conv_b, w_down):
    S = x.shape[1]
    gu = x @ w_gu
    gate, up = gu[..., :D_FF], gu[..., D_FF:]
    half = CONV_WIDTH // 2
    gp = jnp.pad(gate, ((0, 0), (half, half), (0, 0)))
    conv = conv_b.astype(gate.dtype) + sum(gp[:, j:j + S] * conv_w[j] for j in range(CONV_WIDTH))
    hmid = jax.nn.gelu(conv.astype(jnp.float32), approximate=False) * up.astype(jnp.float32)
    return hmid.astype(x.dtype) @ w_down


def _trunk(x, w_in, gate_bias, lam_q1, lam_k1, lam_q2, lam_k2, att_norm_g, lstm_norm_g, w_out,
           ln1_g, ln1_b, w_gu, conv_w, conv_b, w_down, ln2_g, ln2_b):
    for l in range(DEPTH):
        lam_init = 0.8 - 0.6 * math.exp(-0.3 * l)
        mix = _mixer(x, w_in[l], gate_bias[l], lam_q1[l], lam_k1[l], lam_q2[l], lam_k2[l],
                     att_norm_g[l], lstm_norm_g[l], w_out[l], lam_init)
        x = _layer_norm(ALPHA * x + mix.astype(x.dtype), ln1_g[l], ln1_b[l])
        ffn = _conv_ffn(x, w_gu[l], conv_w[l], conv_b[l], w_down[l])
        x = _layer_norm(ALPHA * x + ffn.astype(x.dtype), ln2_g[l], ln2_b[l])
    return x


def setup_inputs(seed: int = 0) -> dict:
    key = jax.random.key(seed)
    ks = jax.random.split(key, 20)
    f32 = jnp.float32
    nrm = lambda k, shape: jax.random.normal(k, shape, f32)
    col = jnp.arange(IN_WIDTH)
    v_mask = ((col >= V_COLS[0][0]) & (col < V_COLS[0][1])) | ((col >= V_COLS[1][0]) & (col < V_COLS[1][1]))
    col_scale = jnp.where(v_mask, BETA, 1.0).astype(f32)
    w_in = nrm(ks[2], (DEPTH, D_MODEL, IN_WIDTH)) * (D_MODEL ** -0.5) * col_scale
    f_off = jnp.linspace(3.0, 6.0, LSTM_HEADS, dtype=f32)
    zero_h = jnp.zeros((LSTM_HEADS,), f32)
    gate_off = jnp.concatenate([zero_h, f_off, zero_h, f_off])
    gate_bias = gate_off + 0.1 * nrm(ks[3], (DEPTH, N_GATES))
    return {
        "x_prompt": nrm(ks[0], (BATCH, SEQ, D_MODEL)),
        "x_sample": nrm(ks[1], (DEC_BATCH, DEC_SEQ, D_MODEL)),
        "w_in": w_in,
        "gate_bias": gate_bias,
        "lam_q1": 0.1 * nrm(ks[4], (DEPTH, ATT_QK_DIM)),
        "lam_k1": 0.1 * nrm(ks[5], (DEPTH, ATT_QK_DIM)),
        "lam_q2": 0.1 * nrm(ks[6], (DEPTH, ATT_QK_DIM)),
        "lam_k2": 0.1 * nrm(ks[7], (DEPTH, ATT_QK_DIM)),
        "att_norm_g": 1.0 + 0.02 * nrm(ks[8], (DEPTH, ATT_WIDTH)),
        "lstm_norm_g": 1.0 + 0.02 * nrm(ks[9], (DEPTH, LSTM_WIDTH)),
        "w_out": nrm(ks[10], (DEPTH, MIX_WIDTH, D_MODEL)) * (MIX_WIDTH ** -0.5) * BETA,
        "ln1_g": 1.0 + 0.02 * nrm(ks[11], (DEPTH, D_MODEL)),
        "ln1_b": 0.02 * nrm(ks[12], (DEPTH, D_MODEL)),
        "w_gu": nrm(ks[13], (DEPTH, D_MODEL, 2 * D_FF)) * (D_MODEL ** -0.5) * BETA,
        "conv_w": nrm(ks[14], (DEPTH, CONV_WIDTH, D_FF)) * (CONV_WIDTH ** -0.5),
        "conv_b": 0.02 * nrm(ks[15], (DEPTH, D_FF)),
        "w_down": nrm(ks[16], (DEPTH, D_FF, D_MODEL)) * (D_FF ** -0.5) * BETA,
        "ln2_g": 1.0 + 0.02 * nrm(ks[17], (DEPTH, D_MODEL)),
        "ln2_b": 0.02 * nrm(ks[18], (DEPTH, D_MODEL)),
    }


def reference(x_prompt, x_sample, w_in, gate_bias, lam_q1, lam_k1, lam_q2, lam_k2, att_norm_g,
              lstm_norm_g, w_out, ln1_g, ln1_b, w_gu, conv_w, conv_b, w_down, ln2_g, ln2_b):
    y_prompt = _trunk(x_prompt, w_in, gate_bias, lam_q1, lam_k1, lam_q2, lam_k2, att_norm_g, lstm_norm_g,
                      w_out, ln1_g, ln1_b, w_gu, conv_w, conv_b, w_down, ln2_g, ln2_b)
    y_sample = _trunk(x_sample, w_in, gate_bias, lam_q1, lam_k1, lam_q2, lam_k2, att_norm_g, lstm_norm_g,
                      w_out, ln1_g, ln1_b, w_gu, conv_w, conv_b, w_down, ln2_g, ln2_b)
    return (y_prompt, y_sample)
```

```python
import math
from contextlib import ExitStack
import numpy as np
import concourse.bass as bass
import concourse.mybir as mybir
from concourse.bass_utils import run_bass_kernel_spmd

F32, BF16, F16 = mybir.dt.float32, mybir.dt.bfloat16, mybir.dt.float16
ALU, AF, AX = mybir.AluOpType, mybir.ActivationFunctionType, mybir.AxisListType

DM = 1024
KC = 8
INW = 3088
DFF = 2816
NFC = 22
DEPTH = 2
ALPHA = (2 * DEPTH) ** 0.25
EPS = 1e-5
SLOPES = [2.0 ** (-8.0 * (h + 1) / 4) for h in range(4)]


def _const_layout():
    off = {}
    c = 0
    for name, w in (("ident", 128), ("bb", 128), ("ba", 128),
                    ("fb", 16), ("fa", 16), ("mf", 128), ("mb", 128), ("sel", 512), ("ones", 128)):
        off[name] = (c, w)
        c += w
    return off, c


BIGOFF = {"dA": (0, 2048), "dB": (2048, 2048), "NI": (4096, 512)}
NBIG = 4608


def _build_consts():
    off, n = _const_layout()
    C = np.zeros((128, n), np.float32)
    CBG = np.zeros((128, NBIG), np.float32)
    j = np.arange(128)[:, None].astype(np.float64)
    C[:, off["ident"][0]:off["ident"][0] + 128] = np.eye(128)
    i = np.arange(512)[None, :].astype(np.float64)
    for d in range(4):
        dist = np.abs(i - 128 * d - j)
        CBG[:, d * 512:(d + 1) * 512] = np.minimum(dist, 256)
        CBG[:, 2048 + d * 512: 2048 + (d + 1) * 512] = np.maximum(dist - 256, 0)
    for h in range(4):
        sl = SLOPES[h]
        CBG[:, 4096 + h * 128: 4096 + (h + 1) * 128] = -sl * np.eye(128)
        for D in range(32):
            C[:, off["bb"][0] + h * 32 + D] = (sl * (j - 128 * D))[:, 0]
            C[:, off["ba"][0] + h * 32 + D] = (-sl * (128 * D + j - 511))[:, 0]
        for k in range(4):
            C[:, off["fb"][0] + h * 4 + k] = np.exp(-sl * (128 * k + j))[:, 0]
            C[:, off["fa"][0] + h * 4 + k] = np.exp(-sl * (511 - 128 * k - j))[:, 0]
    jj = np.arange(128)[None, :]
    C[:, off["mf"][0]:off["mf"][0] + 128] = (j <= jj)
    C[:, off["mb"][0]:off["mb"][0] + 128] = (j >= jj)
    for r in range(8):
        p = r if r < 4 else 32 + (r - 4)
        C[p, off["sel"][0] + r * 64: off["sel"][0] + (r + 1) * 64] = 1.0
    C[:, off["ones"][0]:off["ones"][0] + 128] = 1.0
    return C, CBG


class Buf:
    __slots__ = ("w", "r", "old")

    def __init__(self):
        self.w = []
        self.r = {}
        self.old = []


class _Rec:
    def __init__(self, eng):
        self._eng = eng
        self.first = None

    def __getattr__(self, name):
        f = getattr(self._eng, name)

        def w(*a, **k):
            r = f(*a, **k)
            if self.first is None:
                self.first = r
            return r
        return w


class Ctx:
    def __init__(self, nc, es):
        self.nc = nc
        self.eng = {"pe": nc.tensor, "act": nc.scalar, "dve": nc.vector, "pool": nc.gpsimd, "sp": nc.sync}
        self.sem = {k: es.enter_context(nc.semaphore("s_" + k)) for k in self.eng}
        self.cnt = {k: 0 for k in self.eng}
        self.seen = {k: {} for k in self.eng}
        self.dq = {}
        for q in ("sp", "pool"):
            self.dq[q] = [[es.enter_context(nc.semaphore("d_%s%d" % (q, i))), 0, "d_%s%d" % (q, i)] for i in range(8)]
        self.dq_i = {"sp": 0, "pool": 0}
        self.dbufs = {}
        self.n_inst = 0

    def D(self, name, idx):
        k = (name, idx)
        b = self.dbufs.get(k)
        if b is None:
            b = self.dbufs[k] = Buf()
        return b

    def _wait(self, e, evs, defer=False):
        seen = self.seen[e]
        best = {}
        for (sem, key, val) in evs:
            if seen.get(key, 0) >= val:
                continue
            if key not in best or best[key][1] < val:
                best[key] = (sem, val)
        items = list(best.items())
        deferred = None
        if defer and items:
            key, (sem, val) = items.pop()
            deferred = (sem, val)
            seen[key] = val
        for key, (sem, val) in items:
            self.eng[e].wait_ge(sem, val)
            seen[key] = val
        return deferred

    def _deps(self, e, reads, writes, pwrites):
        evs = []
        for b in reads:
            evs.extend(b.w)
        for b in writes:
            for ev in b.w:
                if ev[1] != e:
                    evs.append(ev)
            for key, ev in b.r.items():
                if key != e:
                    evs.append(ev)
            for ev in b.old:
                if ev[1] != e:
                    evs.append(ev)
        for b in pwrites:
            for ev in b.old:
                if ev[1] != e:
                    evs.append(ev)
        if e == "pe":
            evs = [ev for ev in evs if ev[1] != "pe"]
        return evs

    def _update(self, ev, key, reads, writes, pwrites):
        for b in reads:
            b.r[key] = ev
        for b in writes:
            b.w = [ev]
            b.r = {}
            b.old = []
        for b in pwrites:
            b.w.append(ev)

    def begin_fill(self, b):
        b.old = list(b.r.values()) + list(b.w)
        b.w = []
        b.r = {}

    def op(self, e, fn, reads=(), writes=(), pwrites=()):
        deferred = self._wait(e, self._deps(e, reads, writes, pwrites), defer=True)
        if deferred is None:
            inst = fn(self.eng[e])
        else:
            rec = _Rec(self.eng[e])
            inst = fn(rec)
            rec.first._wait_ge(deferred[0], deferred[1])
        self.cnt[e] += 1
        inst.then_inc(self.sem[e], 1)
        ev = (self.sem[e], e, self.cnt[e])
        self._update(ev, e, reads, writes, pwrites)
        self.n_inst += 1
        return ev

    def dma(self, q, out, in_, reads=(), writes=(), pwrites=()):
        evs = self._deps("__dma__", reads, writes, pwrites)
        slot = self.dq[q][self.dq_i[q] % 8]
        self.dq_i[q] += 1
        if slot[1] > 0:
            evs = evs + [(slot[0], slot[2], slot[1])]
        deferred = self._wait(q, evs, defer=True)
        di = self.eng[q].dma_start(out=out, in_=in_)
        if deferred is not None:
            di._wait_ge(deferred[0], deferred[1])
        di.then_inc(slot[0], 16)
        slot[1] += 16
        ev = (slot[0], slot[2], slot[1])
        self._update(ev, slot[2] + "_%d" % slot[1], reads, writes, pwrites)
        self.n_inst += 1
        return ev

    def barrier(self):
        evs = [(self.sem[k], k, self.cnt[k]) for k in self.eng if self.cnt[k] > 0]
        for q in self.dq:
            for slot in self.dq[q]:
                if slot[1] > 0:
                    evs.append((slot[0], slot[2], slot[1]))
        for e in self.eng:
            self._wait(e, [ev for ev in evs if ev[1] != e])


def build(seq_lens, n_layers=DEPTH, debug=None):
    T = sum(seq_lens)
    SMAX = max(seq_lens)
    NCH_MAX = SMAX // 128
    seqs = []
    o = 0
    for S in seq_lens:
        assert S % 512 == 0
        seqs.append((o, S))
        o += S
    coff, NCONST = _const_layout()

    nc = bass.Bass("TRN2", target_bir_lowering=False)

    def din(name, shape, dt=F32):
        return nc.dram_tensor(name, shape, dt, kind="ExternalInput").ap()

    def dscr(name, shape, dt):
        return nc.dram_tensor(name, shape, dt, kind="Internal").ap()

    x_in = din("x", [T, DM])
    consts_d = din("consts", [128, NCONST])
    constsb_d = din("constsb", [128, NBIG])
    w_in_d = din("w_in", [DEPTH, DM, INW])
    gate_bias_d = din("gate_bias", [DEPTH, 16])
    lam_d = [din(n, [DEPTH, 64]) for n in ("lam_q1", "lam_k1", "lam_q2", "lam_k2")]
    att_g_d = din("att_norm_g", [DEPTH, 512])
    lstm_g_d = din("lstm_norm_g", [DEPTH, 512])
    w_out_d = din("w_out", [DEPTH, DM, DM])
    ln1_g_d = din("ln1_g", [DEPTH, DM])
    ln1_b_d = din("ln1_b", [DEPTH, DM])
    w_gu_d = din("w_gu", [DEPTH, DM, 2 * DFF])
    conv_w_d = din("conv_w", [DEPTH, 3, DFF])
    conv_b_d = din("conv_b", [DEPTH, DFF])
    w_down_d = din("w_down", [DEPTH, DFF, DM])
    ln2_g_d = din("ln2_g", [DEPTH, DM])
    ln2_b_d = din("ln2_b", [DEPTH, DM])
    y_out = nc.dram_tensor("y", [T, DM], F32, kind="ExternalOutput").ap()

    QT = dscr("QT", [4, 128, T], BF16)
    KT = dscr("KT", [4, 128, T], BF16)
    VA = dscr("VA", [T, 512], BF16)
    LQT = dscr("LQT", [2, 128, T], BF16)
    LKT = dscr("LKT", [2, 128, T], BF16)
    LK = dscr("LK", [T, 256], BF16)
    LV = dscr("LV", [T, 512], BF16)
    LO = dscr("LO", [T, 512], F32)
    IGd = dscr("IGd", [8, T], F32)
    FGd = dscr("FGd", [8, T], F32)
    MIX = (nc.dram_tensor("MIX", [T, DM], BF16, kind="ExternalOutput").ap() if debug else dscr("MIX", [T, DM], BF16))
    X1 = dscr("X1", [T, DM], F32)
    X2 = dscr("X2", [T, DM], F32)
    WGUd = dscr("WGUd", [DEPTH, NFC, 128, 8 * 256], BF16)

    uid = [0]

    with ExitStack() as es:
        cx = Ctx(nc, es)

        def SB(st, shape, dt):
            uid[0] += 1
            return st.enter_context(nc.sbuf_tensor("sb%d" % uid[0], shape, dt))

        def PS(st, shape, dt=F32):
            uid[0] += 1
            return st.enter_context(nc.psum_tensor("ps%d" % uid[0], shape, dt))

        block = es.enter_context(nc.Block())

        def main(_e):
            CF = SB(es, [128, NCONST], F32)
            B_CF = Buf()
            cx.dma("sp", CF[:], consts_d[:, :], writes=[B_CF])

            def cf(name, a=0, b=None):
                c0, w = coff[name]
                return CF[:, c0 + a: c0 + (w if b is None else b)]

            ident_f = cf("ident")
            CB = SB(es, [128, 512 + 128 + 128], BF16)
            CH = SB(es, [128, 2048 + 512], F16)
            B_CH = Buf()
            B_CB = Buf()
            cx.begin_fill(B_CB)
            with ExitStack() as ph:
                CBF = SB(ph, [128, NBIG], F32)
                B_CBF = Buf()
                cx.dma("sp", CBF[:], constsb_d[:, :], writes=[B_CBF])
                cx.op("dve", lambda e: e.tensor_tensor(out=CH[:, 0:2048], in0=CBF[:, 0:2048], in1=CBF[:, 2048:4096], op=ALU.add),
                      reads=[B_CBF], writes=[B_CH])
                cx.op("dve", lambda e: e.tensor_copy(out=CH[:, 2048:2560], in_=CBF[:, 4096:4608]), reads=[B_CBF, B_CH], pwrites=[B_CH])
                cx.op("dve", lambda e: e.tensor_copy(out=CB[:, 0:512], in_=CBF[:, 4096:4608]), reads=[B_CBF], pwrites=[B_CB])
                cx.op("dve", lambda e: e.tensor_copy(out=CB[:, 512:640], in_=cf("ones")), reads=[B_CF], pwrites=[B_CB])
                cx.op("dve", lambda e: e.tensor_copy(out=CB[:, 640:768], in_=cf("ident")), reads=[B_CF], pwrites=[B_CB])
                cx.barrier()
            NI_b = lambda h: CB[:, h * 128:(h + 1) * 128]
            ones_b = CB[:, 512:640]
            ident_b = CB[:, 640:768]
            CONSTS = [B_CF, B_CB, B_CH]
            D16 = lambda d: CH[:, d * 512:(d + 1) * 512]
            NI16 = lambda h: CH[:, 2048 + h * 128: 2048 + (h + 1) * 128]

            LNG = SB(es, [128, 4, DM], F32)
            GN = SB(es, [128, 2, 512], F32)
            LAMV = SB(es, [128, 4, 64], F32)
            LAMC = SB(es, [128, 8], F32)
            GBC = SB(es, [64, 2], F32)
            CW = SB(es, [128, NFC, 4], F32)
            B_LNG, B_GN, B_LAM, B_GBC, B_CWR, B_CW = Buf(), Buf(), Buf(), Buf(), Buf(), Buf()
            NT512 = T // 512
            NM = SB(es, [128, 8, NT512], F32)
            BLK = SB(es, [128, 128], BF16)
            HSEL = SB(es, [128, 2, 128], F32)
            B_BLK = Buf()
            cx.op('pool', lambda e: e.memset(BLK[:], 0.0), writes=[B_BLK])
            cx.op('pool', lambda e: e.memset(BLK[0:64, 0:64], 1.0), reads=[B_BLK], pwrites=[B_BLK])
            cx.op('pool', lambda e: e.memset(BLK[64:128, 64:128], 1.0), reads=[B_BLK], pwrites=[B_BLK])
            cx.op('pool', lambda e: e.memset(HSEL[:], 0.0), reads=[B_BLK], pwrites=[B_BLK])
            cx.op('pool', lambda e: e.memset(HSEL[0:64, 0, :], 1.0 / 64.0), reads=[B_BLK], pwrites=[B_BLK])
            cx.op('pool', lambda e: e.memset(HSEL[64:128, 1, :], 1.0 / 64.0), reads=[B_BLK], pwrites=[B_BLK])
            B_NM = Buf()

            rr = [0]

            def evac_eng():
                rr[0] += 1
                return "act" if rr[0] % 2 else "dve"

            def copy_op(e, out, in_, reads, writes=(), pwrites=(), scale=None):
                if e == "act":
                    if scale is None:
                        return cx.op("act", lambda g: g.copy(out=out, in_=in_), reads=reads, writes=writes, pwrites=pwrites)
                    return cx.op("act", lambda g: g.mul(out=out, in_=in_, mul=scale), reads=reads, writes=writes, pwrites=pwrites)
                if scale is None:
                    return cx.op(e, lambda g: g.tensor_copy(out=out, in_=in_), reads=reads, writes=writes, pwrites=pwrites)
                return cx.op(e, lambda g: g.tensor_scalar(out=out, in0=in_, scalar1=float(scale), scalar2=None, op0=ALU.mult),
                             reads=reads, writes=writes, pwrites=pwrites)

            def wgu_convert(ph):
                stf = [SB(ph, [128, 8, 256], F32) for _ in range(2)]
                stb_ = [SB(ph, [128, 8, 256], BF16) for _ in range(2)]
                B_f = [Buf(), Buf()]
                B_b = [Buf(), Buf()]
                it = 0
                for l_ in range(n_layers):
                    for fc in range(NFC):
                        s_ = it % 2
                        it += 1
                        cx.begin_fill(B_f[s_])
                        cx.dma("sp", stf[s_][:, :, 0:128],
                               w_gu_d[l_, :, fc * 128:(fc + 1) * 128].rearrange("(k p) c -> p k c", p=128), pwrites=[B_f[s_]])
                        cx.dma("sp", stf[s_][:, :, 128:256],
                               w_gu_d[l_, :, DFF + fc * 128: DFF + (fc + 1) * 128].rearrange("(k p) c -> p k c", p=128),
                               pwrites=[B_f[s_]])
                        yield
                        copy_op(evac_eng(), stb_[s_][:], stf[s_][:], reads=[B_f[s_]], writes=[B_b[s_]])
                        cx.dma("pool", WGUd[l_, fc], stb_[s_][:].rearrange("p k c -> p (k c)"), reads=[B_b[s_]],
                               writes=[cx.D("WGU", (l_, fc))])
                        yield

            for l in range(n_layers):
                lam_init = 0.8 - 0.6 * math.exp(-0.3 * l)
                x_src = x_in if l == 0 else X2
                x_src_name = "xin" if l == 0 else "X2"
                y_dst = y_out if l == n_layers - 1 else X2
                y_dst_name = "yout" if l == n_layers - 1 else "X2"

                cx.begin_fill(B_LNG)
                for i, d in enumerate((ln1_g_d, ln1_b_d, ln2_g_d, ln2_b_d)):
                    cx.dma("sp", LNG[:, i, :], d[l, :].partition_broadcast(128), pwrites=[B_LNG])
                cx.begin_fill(B_GN)
                cx.dma("sp", GN[:, 0, :], att_g_d[l, :].partition_broadcast(128), pwrites=[B_GN])
                cx.dma("sp", GN[:, 1, :], lstm_g_d[l, :].partition_broadcast(128), pwrites=[B_GN])
                cx.op("dve", lambda e: e.tensor_scalar(out=GN[:, 0, :], in0=GN[:, 0, :], scalar1=float(1.0 - lam_init),
                                                       scalar2=None, op0=ALU.mult), reads=[B_GN], pwrites=[B_GN])
                cx.begin_fill(B_LAM)
                for i in range(4):
                    cx.dma("sp", LAMV[:, i, :], lam_d[i][l, :].partition_broadcast(128), pwrites=[B_LAM])
                B_LAMC = Buf()
                cx.op("dve", lambda e: e.tensor_tensor(out=LAMV[:, 0, :], in0=LAMV[:, 0, :], in1=LAMV[:, 1, :], op=ALU.mult),
                      reads=[B_LAM], pwrites=[B_LAM])
                cx.op("dve", lambda e: e.tensor_tensor(out=LAMV[:, 2, :], in0=LAMV[:, 2, :], in1=LAMV[:, 3, :], op=ALU.mult),
                      reads=[B_LAM], pwrites=[B_LAM])
                cx.op("dve", lambda e: e.tensor_reduce(out=LAMC[:, 1:2], in_=LAMV[:, 0, :], axis=AX.X, op=ALU.add),
                      reads=[B_LAM], writes=[B_LAMC])
                cx.op("dve", lambda e: e.tensor_reduce(out=LAMC[:, 2:3], in_=LAMV[:, 2, :], axis=AX.X, op=ALU.add),
                      reads=[B_LAMC, B_LAM], pwrites=[B_LAMC])
                cx.op("act", lambda e: e.activation(out=LAMC[:, 3:5], in_=LAMC[:, 1:3], func=AF.Exp), reads=[B_LAMC],
                      pwrites=[B_LAMC])
                cx.op("dve", lambda e: e.tensor_tensor(out=LAMC[:, 5:6], in0=LAMC[:, 4:5], in1=LAMC[:, 3:4], op=ALU.subtract),
                      reads=[B_LAMC], pwrites=[B_LAMC])
                cx.op("dve", lambda e: e.tensor_scalar(out=LAMC[:, 0:1], in0=LAMC[:, 5:6], scalar1=float(-lam_init),
                                                       scalar2=None, op0=ALU.add), reads=[B_LAMC], pwrites=[B_LAMC])
                NEGLAM = LAMC[:, 0:1]
                cx.op("pool", lambda e: e.memset(GBC[:], 0.0), writes=[B_GBC])
                for (c, r0, g0) in ((0, 0, 0), (0, 32, 8), (1, 0, 4), (1, 32, 12)):
                    cx.dma("sp", GBC[r0:r0 + 4, c:c + 1], gate_bias_d[l, g0:g0 + 4].rearrange("(p o) -> p o", o=1),
                           reads=[B_GBC], pwrites=[B_GBC])
                with ExitStack() as ph:
                    CWR = SB(ph, [4, DFF], F32)
                    B_CWR = Buf()
                    cx.begin_fill(B_CWR)
                    cx.dma("sp", CWR[0:3, :], conv_w_d[l, :, :], pwrites=[B_CWR])
                    cx.dma("sp", CWR[3:4, :], conv_b_d[l:l + 1, :], pwrites=[B_CWR])
                    pcw = PS(ph, [128, 512])
                    B_pcw = Buf()

                    def cw_t(e):
                        last = None
                        for fc in range(NFC):
                            last = e.transpose(out=pcw[:, fc * 4:(fc + 1) * 4], in_=CWR[0:4, fc * 128:(fc + 1) * 128],
                                               identity=ident_f[0:4, 0:4])
                        return last
                    cx.op("pe", cw_t, reads=[B_CWR] + CONSTS, writes=[B_pcw])
                    cx.op("dve", lambda e: e.tensor_copy(out=CW[:].rearrange("p f c -> p (f c)"), in_=pcw[:, 0:NFC * 4]),
                          reads=[B_pcw], writes=[B_CW])
                    cx.barrier()

                with ExitStack() as ph:
                    ph.enter_context(nc.named_scope('P%d' % l))
                    Win = SB(ph, [128, KC, INW], BF16)
                    WG = SB(ph, [128, KC, 2, 36], BF16)
                    wst = [SB(ph, [128, KC, 256], F32) for _ in range(2)]
                    xs = [SB(ph, [128, 4, DM], F32) for _ in range(2)]
                    xT = [SB(ph, [128, KC, 512], BF16) for _ in range(2)]
                    stb = [SB(ph, [128, 512], BF16) for _ in range(6)]
                    stf = [SB(ph, [128, 512], F32) for _ in range(4)]
                    ptr = [PS(ph, [128, 512]) for _ in range(2)]
                    pmm = [PS(ph, [128, 512]) for _ in range(4)]
                    PN = [PS(ph, [128, 512]) for _ in range(2)]
                    SQ = [SB(ph, [128, 512], BF16) for _ in range(2)]
                    B_PN, B_SQ = [Buf(), Buf()], [Buf(), Buf()]
                    cx.begin_fill(B_NM)
                    B_Win, B_WG = Buf(), Buf()
                    B_wst = [Buf(), Buf()]
                    B_xs = [Buf(), Buf()]
                    B_xT = [Buf(), Buf()]
                    B_stb = [Buf() for _ in range(6)]
                    B_stf = [Buf() for _ in range(4)]
                    B_ptr = [Buf(), Buf()]
                    B_pmm = [Buf() for _ in range(4)]
                    cx.begin_fill(B_Win)
                    for ci, c0 in enumerate(range(0, INW, 256)):
                        w = min(256, INW - c0)
                        s = ci % 2
                        cx.dma("sp", wst[s][:, :, 0:w], w_in_d[l, :, c0:c0 + w].rearrange("(k p) c -> p k c", p=128),
                               writes=[B_wst[s]])
                        copy_op(("act", "dve", "pool")[ci % 3], Win[:, :, c0:c0 + w], wst[s][:, :, 0:w], reads=[B_wst[s]],
                                pwrites=[B_Win])
                    cx.op("pool", lambda e: e.memset(WG[:], 0.0), writes=[B_WG])
                    for (g, r0, c0) in ((0, 0, 0), (0, 32, 8), (1, 0, 4), (1, 32, 12)):
                        cx.op("dve", lambda e, g=g, r0=r0, c0=c0: e.tensor_copy(
                            out=WG[:, :, g, r0:r0 + 4], in_=Win[:, :, 3072 + c0:3072 + c0 + 4]),
                            reads=[B_Win, B_WG], pwrites=[B_WG])
                    ib, jf, im = [0], [0], [0]

                    def mm_group(out_ap, lhs_fn, rhs_fn, reads):
                        i = im[0] % 4
                        im[0] += 1
                        pt = pmm[i]

                        def f(e):
                            last = None
                            for kc in range(KC):
                                last = e.matmul(out_ap(pt), lhsT=lhs_fn(kc), rhs=rhs_fn(kc), start=(kc == 0), stop=(kc == KC - 1))
                            return last
                        cx.op("pe", f, reads=reads, writes=[B_pmm[i]])
                        return pt, B_pmm[i]

                    wgen = wgu_convert(ph) if l == 0 else None
                    pend = []

                    def norm_item(fi, i, ti):
                        z = fi % 2
                        cx.op("act", lambda e: e.activation(out=SQ[z][:, :], in_=stb[i][:, :], func=AF.Square),
                              reads=[B_stb[i]], writes=[B_SQ[z]])

                        cx.op("pe", lambda e: e.matmul(PN[z][:, :], lhsT=BLK[:, :], rhs=SQ[z][:, :], start=True, stop=True),
                              reads=[B_SQ[z], B_BLK], writes=[B_PN[z]])
                        qk, hh = fi // 4, fi % 4
                        col = qk * 4 + hh
                        cx.op("dve", lambda e: e.tensor_reduce(out=NM[:, col, ti:ti + 1], in_=PN[z][:, :], axis=AX.X, op=ALU.max),
                              reads=[B_PN[z]], pwrites=[B_NM])

                    def prep(ti):
                        t0 = ti * 512
                        s = ti % 2
                        cx.dma("sp", xs[s][:], x_src[t0:t0 + 512, :].rearrange("(a p) d -> p a d", p=128),
                               reads=[cx.D(x_src_name, ti)], writes=[B_xs[s]])
                        cx.begin_fill(B_xT[s])
                        for kc in range(KC):
                            pi = kc % 2

                            def tr(e, kc=kc, pi=pi, s=s):
                                last = None
                                for a in range(4):
                                    last = e.transpose(out=ptr[pi][:, a * 128:(a + 1) * 128],
                                                       in_=xs[s][:, a, kc * 128:(kc + 1) * 128], identity=ident_f)
                                return last
                            cx.op("pe", tr, reads=[B_xs[s]] + CONSTS, writes=[B_ptr[pi]])
                            copy_op(evac_eng(), xT[s][:, kc, :], ptr[pi][:, :], reads=[B_ptr[pi]], pwrites=[B_xT[s]])

                    prep(0)
                    for ti in range(T // 512):
                        t0 = ti * 512
                        s = ti % 2
                        xTs = xT[s]
                        fm = []
                        for h in range(4):
                            fm.append((h * 128, 0.125, QT[h, :, t0:t0 + 512], "QT"))
                        for h in range(4):
                            fm.append((512 + h * 128, None, KT[h, :, t0:t0 + 512], "KT"))
                        for p in range(2):
                            fm.append((1536 + p * 128, None, LQT[p, :, t0:t0 + 512], "LQT"))
                        for p in range(2):
                            fm.append((1792 + p * 128, 0.125, LKT[p, :, t0:t0 + 512], "LKT"))
                        for fi, (c0, sc, dst, dn) in enumerate(fm):
                            pt, bpt = mm_group(lambda pt: pt[:, 0:512], lambda kc, c0=c0: Win[:, kc, c0:c0 + 128],
                                               lambda kc: xTs[:, kc, :], [B_Win, B_xT[s]])
                            i = ib[0] % 6
                            ib[0] += 1
                            copy_op(evac_eng(), stb[i][:, :], pt[:, 0:512], reads=[bpt], writes=[B_stb[i]], scale=sc)
                            cx.dma("pool", dst, stb[i][:, :], reads=[B_stb[i]], pwrites=[cx.D(dn, ti)])
                            if fi == 7 and ti + 1 < T // 512:
                                prep(ti + 1)
                            if wgen is not None and fi % 2 == 1:
                                try:
                                    next(wgen)
                                except StopIteration:
                                    wgen = None
                            if fi < 8:
                                pend.append((fi, i))
                            if len(pend) > 2:
                                norm_item(*pend.pop(0), ti)
                        while pend:
                            norm_item(*pend.pop(0), ti)
                        for g, dst in ((0, IGd), (1, FGd)):
                            pt, bpt = mm_group(lambda pt: pt[0:36, 0:512], lambda kc, g=g: WG[:, kc, g, :],
                                               lambda kc: xTs[:, kc, :], [B_WG, B_xT[s]])
                            j = jf[0] % 4
                            jf[0] += 1
                            cx.op("dve", lambda e, j=j, pt=pt, g=g: e.tensor_scalar(
                                out=stf[j][0:36, :], in0=pt[0:36, 0:512], scalar1=GBC[0:36, g:g + 1], scalar2=None, op0=ALU.add),
                                reads=[bpt, B_GBC], writes=[B_stf[j]])
                            cx.dma("pool", dst[0:4, t0:t0 + 512], stf[j][0:4, :], reads=[B_stf[j]], pwrites=[cx.D("G", ti)])
                            cx.dma("pool", dst[4:8, t0:t0 + 512], stf[j][32:36, :], reads=[B_stf[j]], pwrites=[cx.D("G", ti)])
                        for a in range(4):
                            r0 = t0 + a * 128
                            for (c0, wd, dst, dn, isf, sc) in ((1024, 512, VA, "VA", False, None), (2048, 512, LV, "LV", False, None),
                                                             (2560, 512, LO, "LO", True, None), (1792, 256, LK, "LK", False, 0.125)):
                                pt, bpt = mm_group(lambda pt, wd=wd: pt[:, 0:wd], lambda kc, a=a: xTs[:, kc, a * 128:(a + 1) * 128],
                                                   lambda kc, c0=c0, wd=wd: Win[:, kc, c0:c0 + wd], [B_Win, B_xT[s]])
                                if isf:
                                    j = jf[0] % 4
                                    jf[0] += 1
                                    copy_op(evac_eng(), stf[j][:, 0:wd], pt[:, 0:wd], reads=[bpt], writes=[B_stf[j]])
                                    cx.dma("pool", dst[r0:r0 + 128, :], stf[j][:, 0:wd], reads=[B_stf[j]], pwrites=[cx.D(dn, ti)])
                                else:
                                    i = ib[0] % 6
                                    ib[0] += 1
                                    copy_op(evac_eng(), stb[i][:, 0:wd], pt[:, 0:wd], reads=[bpt], writes=[B_stb[i]], scale=sc)
                                    cx.dma("pool", dst[r0:r0 + 128, :], stb[i][:, 0:wd], reads=[B_stb[i]], pwrites=[cx.D(dn, ti)])
                    if wgen is not None:
                        for _ in wgen:
                            pass
                    cx.barrier()

                def dtiles(name, s_off, S):
                    return [cx.D(name, ti) for ti in range(s_off // 512, (s_off + S) // 512)]

                with ExitStack() as ph:
                    ph.enter_context(nc.named_scope('A%d' % l))
                    Qs = [SB(ph, [128, SMAX], BF16) for _ in range(2)]
                    Ks = [SB(ph, [128, SMAX], BF16) for _ in range(2)]
                    Vs = [SB(ph, [128, NCH_MAX, 129], BF16) for _ in range(2)]
                    Ptp = [SB(ph, [128, 1024], BF16) for _ in range(3)]
                    Pt = [[Ptp[i][:, mp * 512:(mp + 1) * 512] for i in range(3)] for mp in range(2)]
                    TMPd = [SB(ph, [128, 3, 8, 129], F32) for _ in range(2)]
                    OAq = [SB(ph, [128, 4, 128], F32) for _ in range(2)]
                    OAs = SB(ph, [128, 4, 128], F32)
                    EPq = [SB(ph, [128, 24], F32) for _ in range(2)]
                    SMQ = SB(ph, [128, 56], F32)
                    BTd = [SB(ph, [128, 2, 2, 32], F32) for _ in range(2)]
                    EP = [SB(ph, [128, 16], F32) for _ in range(2)]
                    OA = [SB(ph, [128, 128], F32) for _ in range(2)]
                    ATS = [SB(ph, [128, 4, 128], BF16) for _ in range(2)]
                    OAsq = [SB(ph, [128, 128], F32) for _ in range(2)]
                    SCp = [PS(ph, [128, 1024]) for _ in range(2)]
                    SC = [[SCp[par][:, mp * 512:(mp + 1) * 512] for par in range(2)] for mp in range(2)]
                    ACC = [PS(ph, [128, 512]) for _ in range(3)]
                    PSM = PS(ph, [128, 512])
                    B_PSM = Buf()
                    B_Q, B_K, B_V = [Buf(), Buf()], [Buf(), Buf()], [Buf(), Buf()]
                    B_Pt = [[Buf() for _ in range(3)] for _ in range(2)]
                    B_MX, B_SM = Buf(), Buf()
                    B_TMPd, B_OAq, B_EPq = [Buf(), Buf()], [Buf(), Buf()], [Buf(), Buf()]
                    B_OAs = Buf()
                    pending_ep = [None]
                    qbc = [0]
                    B_BTd = [Buf(), Buf()]
                    B_EP = [Buf(), Buf()]
                    B_OA = [Buf(), Buf()]
                    B_ATS = [Buf(), Buf()]
                    B_OAsq = [Buf(), Buf()]
                    B_SC = [[Buf(), Buf()], [Buf(), Buf()]]
                    B_ACC = [Buf(), Buf(), Buf()]
                    for s in range(2):
                        cx.op("pool", lambda e, s=s: e.memset(Vs[s][:, :, 128:129], 1.0), pwrites=[B_V[s]])
                    hi = 0
                    bbt = lambda h: CF[:, coff["bb"][0] + h * 32: coff["bb"][0] + (h + 1) * 32]
                    bat = lambda h: CF[:, coff["ba"][0] + h * 32: coff["ba"][0] + (h + 1) * 32]
                    fbc = lambda h, k: CF[:, coff["fb"][0] + h * 4 + k: coff["fb"][0] + h * 4 + k + 1]
                    fac = lambda h, k: CF[:, coff["fa"][0] + h * 4 + k: coff["fa"][0] + h * 4 + k + 1]
                    for (s_off, S) in seqs:
                        nqb, nst = S // 512, S // 128
                        ti0, ti1 = s_off // 512, (s_off + S) // 512
                        cx.op("dve", lambda e: e.tensor_reduce(out=SMQ[:, 0:8], in_=NM[:, :, ti0:ti1], axis=AX.X, op=ALU.max),
                              reads=[B_NM], writes=[B_SM])
                        cx.op("dve", lambda e: e.tensor_tensor(out=SMQ[:, 8:12], in0=SMQ[:, 0:4], in1=SMQ[:, 4:8], op=ALU.mult),
                              reads=[B_SM], pwrites=[B_SM])

                        def hb(e):
                            e.matmul(PSM[:, 0:4], lhsT=HSEL[:, 0, :], rhs=SMQ[:, 8:12], start=True, stop=True, skip_group_check=True)
                            return e.matmul(PSM[:, 4:8], lhsT=HSEL[:, 1, :], rhs=SMQ[:, 8:12], start=True, stop=True,
                                            skip_group_check=True)
                        cx.op("pe", hb, reads=[B_SM, B_BLK], writes=[B_PSM])
                        cx.op("dve", lambda e: e.tensor_copy(out=SMQ[:, 16:24], in_=PSM[:, 0:8]), reads=[B_PSM, B_SM], pwrites=[B_SM])
                        cx.op("act", lambda e: e.activation(out=SMQ[:, 24:32], in_=SMQ[:, 16:24], func=AF.Ln), reads=[B_SM],
                              pwrites=[B_SM])
                        cx.op("act", lambda e: e.activation(out=SMQ[:, 32:40], in_=SMQ[:, 24:32], func=AF.Exp, scale=0.5),
                              reads=[B_SM], pwrites=[B_SM])
                        cx.op("dve", lambda e: e.tensor_scalar(out=SMQ[:, 40:48], in0=SMQ[:, 32:40], scalar1=-1.0, scalar2=None,
                                                               op0=ALU.mult), reads=[B_SM], pwrites=[B_SM])
                        cx.op("dve", lambda e: e.tensor_tensor(out=SMQ[:, 48:52], in0=SMQ[:, 40:44], in1=SMQ[:, 44:48], op=ALU.min),
                              reads=[B_SM], pwrites=[B_SM])
                        for h in range(4):
                            s = hi % 2
                            hi += 1
                            Q, K, V = Qs[s], Ks[s], Vs[s]
                            BT, B_BT = BTd[s], B_BTd[s]
                            cx.dma("sp", Q[:, 0:S], QT[h, :, s_off:s_off + S], reads=dtiles("QT", s_off, S), writes=[B_Q[s]])
                            cx.dma("sp", K[:, 0:S], KT[h, :, s_off:s_off + S], reads=dtiles("KT", s_off, S), writes=[B_K[s]])
                            cx.begin_fill(B_V[s])
                            for c0 in range(0, nst, 8):
                                c1 = min(c0 + 8, nst)
                                cx.dma("sp", V[:, c0:c1, 0:128],
                                       VA[s_off + c0 * 128:s_off + c1 * 128, h * 128:(h + 1) * 128].rearrange(
                                           "(c p) d -> p c d", p=128),
                                       reads=dtiles("VA", s_off, S), pwrites=[B_V[s]])
                            cx.begin_fill(B_BT)
                            for mp in range(2):
                                cx.op("dve", lambda e, mp=mp: e.tensor_scalar(out=BT[:, 0, mp, :], in0=bbt(h),
                                                                            scalar1=SMQ[:, 48 + h:49 + h], scalar2=None, op0=ALU.add),
                                      reads=[B_SM] + CONSTS, pwrites=[B_BT])
                                cx.op("dve", lambda e, mp=mp: e.tensor_scalar(out=BT[:, 1, mp, :], in0=bat(h),
                                                                            scalar1=SMQ[:, 48 + h:49 + h], scalar2=None, op0=ALU.add),
                                      reads=[B_SM] + CONSTS, pwrites=[B_BT])
                            pre_qk = [False]
                            for qb in range(nqb):
                                units = []
                                for ts in range(0, 4 * qb):
                                    units.append(("b", ts))
                                for ts in range(4 * qb, 4 * qb + 4):
                                    units.append(("i", ts))
                                for ts in range(4 * qb + 4, nst):
                                    units.append(("a", ts))
                                nu = len(units)
                                qsl = slice(qb * 512, (qb + 1) * 512)

                                def emit_qk(i):
                                    kind, ts = units[i]
                                    par = i % 2
                                    ksl = slice(ts * 128, (ts + 1) * 128)

                                    def f(e):
                                        fin = (kind != "i")
                                        e.matmul(SC[0][par][:, :], lhsT=K[0:64, ksl], rhs=Q[0:64, qsl], start=True, stop=fin)
                                        last = e.matmul(SC[1][par][:, :], lhsT=K[64:128, ksl], rhs=Q[64:128, qsl], start=True,
                                                        stop=fin)
                                        if kind == "i":
                                            d = ts - 4 * qb
                                            for mp in range(2):
                                                last = e.matmul(SC[mp][par][:, :], lhsT=NI16(h), rhs=D16(d), start=False, stop=True,
                                                                skip_group_check=True)
                                        return last
                                    cx.op("pe", f, reads=[B_Q[s], B_K[s]] + CONSTS, writes=[B_SC[0][par], B_SC[1][par]])

                                def emit_exp(i):
                                    kind, ts = units[i]
                                    par = i % 2
                                    pi = i % 3
                                    if kind == "b":
                                        bias = BT[:, 0, 0, 4 * qb - ts: 4 * qb - ts + 1]
                                    elif kind == "a":
                                        bias = BT[:, 1, 0, ts - 4 * qb: ts - 4 * qb + 1]
                                    else:
                                        bias = SMQ[:, 48 + h:49 + h]
                                    cx.op("act", lambda e: e.activation(out=Ptp[pi][:, :], in_=SCp[par][:, :], func=AF.Exp, bias=bias,
                                                                        scale=1.0),
                                          reads=[B_SC[0][par], B_SC[1][par], B_BT, B_SM], writes=[B_Pt[0][pi], B_Pt[1][pi]])

                                def emit_av(i, first, lastu):
                                    kind, ts = units[i]
                                    pi = i % 3

                                    def f(e):
                                        last = None
                                        for k in range(4):
                                            for mp in range(2):
                                                idx = k * 2 + mp
                                                bank, col = idx // 3, (idx % 3) * 129
                                                last = e.matmul(ACC[bank][:, col:col + 129], lhsT=Pt[mp][pi][:, k * 128:(k + 1) * 128],
                                                                rhs=V[:, ts, :], start=(first and idx % 3 == 0), stop=lastu,
                                                                skip_group_check=True)
                                        return last
                                    cx.op("pe", f, reads=[B_Pt[0][pi], B_Pt[1][pi], B_V[s]], writes=B_ACC)

                                def emit_phase_end(kind):
                                    phi = "bia".index(kind)
                                    for bank in range(3):
                                        n = 3 if bank < 2 else 2
                                        cx.op("dve", lambda e, bank=bank, n=n, phi=phi: e.tensor_copy(
                                            out=TMP[:, phi, 3 * bank:3 * bank + n, :].rearrange("p a c -> p (a c)"),
                                            in_=ACC[bank][:, 0:n * 129]), reads=[B_ACC[bank]], pwrites=[B_TMP])

                                zq = qbc[0] % 2
                                qbc[0] += 1
                                TMP, B_TMP = TMPd[zq], B_TMPd[zq]
                                cx.begin_fill(B_TMP)
                                has_b = qb > 0
                                has_a = 4 * qb + 4 < nst
                                if not pre_qk[0]:
                                    emit_qk(0)
                                pre_qk[0] = False
                                deferred_av = []
                                for i in range(nu):
                                    kind = units[i][0]
                                    emit_exp(i)
                                    if i + 1 < nu:
                                        emit_qk(i + 1)
                                    elif qb + 1 < nqb:
                                        qn = slice((qb + 1) * 512, (qb + 2) * 512)

                                        def fqn(e, qn=qn):
                                            e.matmul(SC[0][0][:, :], lhsT=K[0:64, 0:128], rhs=Q[0:64, qn], start=True, stop=True)
                                            return e.matmul(SC[1][0][:, :], lhsT=K[64:128, 0:128], rhs=Q[64:128, qn], start=True, stop=True)
                                        cx.op("pe", fqn, reads=[B_Q[s], B_K[s]] + CONSTS, writes=[B_SC[0][0], B_SC[1][0]])
                                        pre_qk[0] = True
                                    for fn in deferred_av:
                                        fn()
                                    deferred_av = []
                                    first = (i == 0) or (units[i - 1][0] != kind)
                                    lastu = (i == nu - 1) or (units[i + 1][0] != kind)

                                    def do_av(i=i, first=first, lastu=lastu, kind=kind):
                                        emit_av(i, first, lastu)
                                        if lastu:
                                            emit_phase_end(kind)
                                    if first and i > 0 and not lastu:
                                        deferred_av.append(do_av)
                                    else:
                                        do_av()
                                    if pending_ep[0] is not None:
                                        try:
                                            next(pending_ep[0])
                                        except StopIteration:
                                            pending_ep[0] = None
                                for fn in deferred_av:
                                    fn()

                                def epilogue(h=h, zq=zq, TMP=TMP, B_TMP=B_TMP, has_b=has_b, has_a=has_a, r0=s_off + qb * 512):
                                    oaq, boaq, epq, bepq = OAq[zq], B_OAq[zq], EPq[zq], B_EPq[zq]
                                    n_ops = 0
                                    for k in range(4):
                                        for mp in range(2):
                                            idx = k * 2 + mp
                                            if has_b:
                                                cx.op("dve", lambda e, idx=idx, k=k: e.scalar_tensor_tensor(
                                                    out=TMP[:, 1, idx, :], in0=TMP[:, 0, idx, :], scalar=fbc(h, k), in1=TMP[:, 1, idx, :],
                                                    op0=ALU.mult, op1=ALU.add), reads=[B_TMP] + CONSTS, pwrites=[B_TMP])
                                            if has_a:
                                                cx.op("dve", lambda e, idx=idx, k=k: e.scalar_tensor_tensor(
                                                    out=TMP[:, 1, idx, :], in0=TMP[:, 2, idx, :], scalar=fac(h, k), in1=TMP[:, 1, idx, :],
                                                    op0=ALU.mult, op1=ALU.add), reads=[B_TMP] + CONSTS, pwrites=[B_TMP])
                                        yield
                                    cx.op("dve", lambda e: e.reciprocal(out=epq[:, 0:8], in_=TMP[:, 1, 0:8, 128]),
                                          reads=[B_TMP], writes=[bepq])
                                    cx.op("dve", lambda e: e.tensor_scalar(out=epq[:, 8:12], in0=epq[:, 1:8:2], scalar1=NEGLAM,
                                                                           scalar2=None, op0=ALU.mult),
                                          reads=[bepq, B_LAMC], pwrites=[bepq])
                                    yield
                                    cx.begin_fill(boaq)
                                    for k in range(4):
                                        cx.op("dve", lambda e, k=k: e.tensor_scalar(
                                            out=oaq[:, k, :], in0=TMP[:, 1, 2 * k, 0:128], scalar1=epq[:, 2 * k:2 * k + 1], scalar2=None,
                                            op0=ALU.mult), reads=[bepq, B_TMP], pwrites=[boaq])
                                        cx.op("dve", lambda e, k=k: e.scalar_tensor_tensor(
                                            out=oaq[:, k, :], in0=TMP[:, 1, 2 * k + 1, 0:128], scalar=epq[:, 8 + k:9 + k], in1=oaq[:, k, :],
                                            op0=ALU.mult, op1=ALU.add), reads=[bepq, B_TMP, boaq], pwrites=[boaq])
                                        yield
                                    cx.op("dve", lambda e: e.tensor_tensor(out=OAs[:, :, :], in0=oaq[:, :, :], in1=oaq[:, :, :],
                                                                           op=ALU.mult), reads=[boaq], writes=[B_OAs])
                                    cx.op("dve", lambda e: e.tensor_reduce(out=epq[:, 12:16], in_=OAs[:, :, :], axis=AX.X, op=ALU.add),
                                          reads=[B_OAs, bepq], pwrites=[bepq])
                                    cx.op("dve", lambda e: e.tensor_scalar(out=epq[:, 16:20], in0=epq[:, 12:16], scalar1=1.0 / 128.0,
                                                                           scalar2=EPS, op0=ALU.mult, op1=ALU.add),
                                          reads=[bepq], pwrites=[bepq])
                                    yield
                                    yield
                                    cx.op("act", lambda e: e.activation(out=epq[:, 16:20], in_=epq[:, 16:20], func=AF.Ln), reads=[bepq],
                                          pwrites=[bepq])
                                    cx.op("act", lambda e: e.activation(out=epq[:, 20:24], in_=epq[:, 16:20], func=AF.Exp, scale=-0.5),
                                          reads=[bepq], pwrites=[bepq])
                                    yield
                                    cx.begin_fill(B_ATS[zq])
                                    for k in range(4):
                                        cx.op("dve", lambda e, k=k: e.scalar_tensor_tensor(
                                            out=ATS[zq][:, k, :], in0=oaq[:, k, :], scalar=epq[:, 20 + k:21 + k],
                                            in1=GN[:, 0, h * 128:(h + 1) * 128], op0=ALU.mult, op1=ALU.mult),
                                            reads=[bepq, boaq, B_GN], pwrites=[B_ATS[zq]])
                                    cx.dma("pool", MIX[r0:r0 + 512, h * 128:(h + 1) * 128].rearrange("(k p) d -> p k d", p=128),
                                           ATS[zq][:, :, :], reads=[B_ATS[zq]], pwrites=[cx.D("MIX", r0 // 512)])

                                if pending_ep[0] is not None:
                                    for _ in pending_ep[0]:
                                        pass
                                pending_ep[0] = epilogue()
                    if pending_ep[0] is not None:
                        for _ in pending_ep[0]:
                            pass
                        pending_ep[0] = None
                    cx.barrier()

                with ExitStack() as ph:
                    ph.enter_context(nc.named_scope('L%d' % l))
                    IG = SB(ph, [64, SMAX], F32)
                    FG = SB(ph, [64, SMAX], F32)
                    TA = SB(ph, [64, SMAX], F32)
                    RM = SB(ph, [64, SMAX], BF16)
                    SML = SB(ph, [64, 8, NCH_MAX], F32)
                    ETM = SB(ph, [128, NCH_MAX, 36], F32)
                    FTM = SB(ph, [128, NCH_MAX, 36], F32)
                    BBC = SB(ph, [128, 8, NCH_MAX], F32)
                    LQs = SB(ph, [128, SMAX], BF16)
                    LKs = SB(ph, [128, SMAX], BF16)
                    LKh = SB(ph, [128, NCH_MAX, 64], BF16)
                    LVh = SB(ph, [128, NCH_MAX, 129], BF16)
                    LOh = SB(ph, [128, NCH_MAX, 128], F32)
                    HD = [SB(ph, [128, NCH_MAX, 129], F32) for _ in range(2)]
                    RD = SB(ph, [128, 2, 4, NCH_MAX], F32)
                    B_HD = [Buf(), Buf()]
                    B_RD = Buf()
                    MXL = SB(ph, [128, NCH_MAX, 128], BF16)
                    CST = SB(ph, [128, 2, 129], F32)
                    CSB = [SB(ph, [128, 2, 129], BF16) for _ in range(2)]
                    SMT = [SB(ph, [128, 128], BF16) for _ in range(4)]
                    KKT = [SB(ph, [128, 64], BF16) for _ in range(4)]
                    DN = [SB(ph, [128, 4], F32) for _ in range(4)]
                    JK = SB(ph, [128, 128], F32)
                    SS = SB(ph, [128, 3, NCH_MAX], F32)
                    T1 = [SB(ph, [128, 128], F32) for _ in range(2)]
                    PSTb = [PS(ph, [128, 512]) for _ in range(2)]
                    POb = [PS(ph, [128, 512]) for _ in range(2)]
                    PUb = [PS(ph, [128, 512]) for _ in range(2)]
                    CSBq = [SB(ph, [128, 129], BF16) for _ in range(4)]
                    B_PSTq = [Buf() for _ in range(2)]
                    B_POq = [Buf() for _ in range(2)]
                    B_PUq = [Buf() for _ in range(2)]
                    B_CSBq = [Buf() for _ in range(4)]
                    B_CSTd = [Buf(), Buf()]
                    PTR = PS(ph, [128, 512])
                    PBB = PS(ph, [128, 512])
                    B_IG, B_FG, B_TA, B_RM, B_SML, B_ETM, B_FTM, B_BBC = (Buf() for _ in range(8))
                    B_LQ, B_LK, B_LKh, B_LVh, B_LOh, B_H_unused, B_MXL, B_CST, B_JK, B_SS = (Buf() for _ in range(10))
                    B_CSB = [Buf(), Buf()]
                    B_SMT = [Buf() for _ in range(4)]
                    B_KKT = [Buf() for _ in range(4)]
                    B_DN = [Buf() for _ in range(4)]
                    B_T1 = [Buf(), Buf()]
                    B_PTR, B_PBB = Buf(), Buf()
                    cx.op("pool", lambda e: e.memset(IG[:], 0.0), writes=[B_IG])
                    cx.op("pool", lambda e: e.memset(FG[:], 0.0), writes=[B_FG])
                    cx.op("pool", lambda e: e.memset(TA[:], 0.0), writes=[B_TA])
                    cx.op("pool", lambda e: e.memset(SML[:], 0.0), writes=[B_SML])
                    cx.op("dve", lambda e: e.memset(RM[:], 1.0), writes=[B_RM])
                    cx.op("dve", lambda e: e.memset(RM[0:4, 0:SMAX:128], 0.0), reads=[B_RM], pwrites=[B_RM])
                    cx.op("dve", lambda e: e.memset(RM[32:36, 127:SMAX:128], 0.0), reads=[B_RM], pwrites=[B_RM])
                    cx.op("pool", lambda e: e.memset(LVh[:, :, 128:129], 1.0), pwrites=[B_LVh])
                    mask_ap = (cf("mf"), cf("mb"))
                    selc = lambda r: CF[0:36, coff["sel"][0] + r * 64: coff["sel"][0] + (r + 1) * 64]
                    stp = [0]
                    for (s_off, S) in seqs:
                        nch = S // 128
                        cx.begin_fill(B_IG)
                        cx.begin_fill(B_FG)
                        cx.dma("sp", IG[0:4, 0:S], IGd[0:4, s_off:s_off + S], reads=dtiles("G", s_off, S), pwrites=[B_IG])
                        cx.dma("sp", IG[32:36, 0:S], IGd[4:8, s_off:s_off + S], reads=dtiles("G", s_off, S), pwrites=[B_IG])
                        cx.dma("sp", FG[0:4, 0:S], FGd[0:4, s_off:s_off + S], reads=dtiles("G", s_off, S), pwrites=[B_FG])
                        cx.dma("sp", FG[32:36, 0:S], FGd[4:8, s_off:s_off + S], reads=dtiles("G", s_off, S), pwrites=[B_FG])
                        A_, F_, I_ = TA[0:36, 0:S], FG[0:36, 0:S], IG[0:36, 0:S]
                        v3 = lambda ap: ap.rearrange("p (c j) -> p c j", j=128)
                        cx.op("dve", lambda e: e.scalar_tensor_tensor(out=A_, in0=F_, scalar=-1.0, in1=F_, op0=ALU.mult, op1=ALU.min),
                              reads=[B_FG], writes=[B_TA])
                        cx.op("act", lambda e: e.activation(out=A_, in_=A_, func=AF.Exp), reads=[B_TA], writes=[B_TA])
                        cx.op("act", lambda e: e.activation(out=A_, in_=A_, func=AF.Ln, bias=1.0), reads=[B_TA], writes=[B_TA])
                        cx.op("dve", lambda e: e.scalar_tensor_tensor(out=F_, in0=F_, scalar=0.0, in1=A_, op0=ALU.min, op1=ALU.subtract),
                              reads=[B_FG, B_TA], writes=[B_FG])
                        cx.op("dve", lambda e: e.tensor_tensor_scan(out=TA[0:4, 0:S], data0=RM[0:4, 0:S], data1=FG[0:4, 0:S],
                                                                    initial=0.0, op0=ALU.mult, op1=ALU.add),
                              reads=[B_FG, B_RM], writes=[B_TA])
                        cx.op("dve", lambda e: e.tensor_tensor_scan(out=TA[32:36, S - 1::-1] if False else TA[32:36, 0:S][:, ::-1],
                                                                    data0=RM[32:36, 0:S][:, ::-1], data1=FG[32:36, 0:S][:, ::-1],
                                                                    initial=0.0, op0=ALU.mult, op1=ALU.add),
                              reads=[B_FG, B_RM, B_TA], pwrites=[B_TA])
                        cx.op("dve", lambda e: e.tensor_tensor(out=I_, in0=I_, in1=A_, op=ALU.subtract), reads=[B_IG, B_TA], writes=[B_IG])
                        WM, GC, MS, MPV, MC, BETA = (SML[0:36, i, 0:nch] for i in range(6))
                        cx.op("dve", lambda e: e.tensor_reduce(out=WM, in_=v3(I_), axis=AX.X, op=ALU.max), reads=[B_IG], writes=[B_SML])
                        cx.op("dve", lambda e: e.tensor_copy(out=SML[0:4, 1, 0:nch], in_=TA[0:4, 127:S:128]), reads=[B_TA, B_SML],
                              pwrites=[B_SML])
                        cx.op("dve", lambda e: e.tensor_copy(out=SML[32:36, 1, 0:nch], in_=TA[32:36, 0:S:128]), reads=[B_TA, B_SML],
                              pwrites=[B_SML])
                        cx.op("dve", lambda e: e.tensor_tensor_scan(out=SML[0:4, 2, 0:nch], data0=SML[0:4, 0, 0:nch],
                                                                    data1=SML[0:4, 1, 0:nch], initial=0.0, op0=ALU.max, op1=ALU.add),
                              reads=[B_SML], pwrites=[B_SML])
                        cx.op("dve", lambda e: e.tensor_tensor_scan(out=SML[32:36, 2, 0:nch][:, ::-1], data0=SML[32:36, 0, 0:nch][:, ::-1],
                                                                    data1=SML[32:36, 1, 0:nch][:, ::-1], initial=0.0, op0=ALU.max,
                                                                    op1=ALU.add), reads=[B_SML], pwrites=[B_SML])
                        cx.op("dve", lambda e: e.memset(SML[0:36, 3, 0:nch], 0.0), reads=[B_SML], pwrites=[B_SML])
                        if nch > 1:
                            cx.op("dve", lambda e: e.tensor_copy(out=SML[0:4, 3, 1:nch], in_=SML[0:4, 2, 0:nch - 1]), reads=[B_SML],
                                  pwrites=[B_SML])
                            cx.op("dve", lambda e: e.tensor_copy(out=SML[32:36, 3, 0:nch - 1], in_=SML[32:36, 2, 1:nch]), reads=[B_SML],
                                  pwrites=[B_SML])
                        cx.op("dve", lambda e: e.tensor_tensor(out=MC, in0=MPV, in1=WM, op=ALU.max), reads=[B_SML], pwrites=[B_SML])
                        cx.op("dve", lambda e: e.tensor_tensor(out=BETA, in0=MPV, in1=MC, op=ALU.subtract), reads=[B_SML], pwrites=[B_SML])
                        cx.op("act", lambda e: e.activation(out=BETA, in_=BETA, func=AF.Exp), reads=[B_SML], pwrites=[B_SML])
                        mcb = MC.unsqueeze(2).to_broadcast([36, nch, 128])
                        cx.op("dve", lambda e: e.tensor_tensor(out=v3(I_), in0=v3(I_), in1=mcb, op=ALU.subtract), reads=[B_IG, B_SML],
                              writes=[B_IG])
                        cx.op("act", lambda e: e.activation(out=I_, in_=I_, func=AF.Exp), reads=[B_IG], writes=[B_IG])
                        cx.op("dve", lambda e: e.scalar_tensor_tensor(out=v3(A_), in0=v3(A_), scalar=-1.0, in1=mcb, op0=ALU.mult,
                                                                      op1=ALU.subtract), reads=[B_TA, B_SML], writes=[B_TA])
                        cx.op("act", lambda e: e.activation(out=A_, in_=A_, func=AF.Exp), reads=[B_TA], writes=[B_TA])
                        for (src, bsrc, dst, bdst) in ((IG, B_IG, ETM, B_ETM), (TA, B_TA, FTM, B_FTM)):
                            cx.begin_fill(bdst)
                            for c0 in range(0, nch, 8):
                                n8 = min(8, nch - c0)

                                def trf(e, src=src, c0=c0, n8=n8):
                                    last = None
                                    for c in range(n8):
                                        last = e.transpose(out=PTR[:, c * 36:(c + 1) * 36], in_=src[0:36, (c0 + c) * 128:(c0 + c + 1) * 128],
                                                           identity=ident_f[0:36, 0:36])
                                    return last
                                cx.op("pe", trf, reads=[bsrc] + CONSTS, writes=[B_PTR])
                                cx.op("dve", lambda e, dst=dst, c0=c0, n8=n8: e.tensor_copy(
                                    out=dst[:, c0:c0 + n8, :].rearrange("p c r -> p (c r)"), in_=PTR[:, 0:n8 * 36]),
                                    reads=[B_PTR], pwrites=[bdst])

                        def bbf(e):
                            last = None
                            for r in range(8):
                                hp = (r % 4) % 2
                                last = e.matmul(PBB[hp * 64:(hp + 1) * 64, r * nch:(r + 1) * nch], lhsT=selc(r), rhs=BETA, start=True,
                                                stop=True, skip_group_check=True)
                            return last
                        cx.op("pe", bbf, reads=[B_SML] + CONSTS, writes=[B_PBB])
                        cx.begin_fill(B_BBC)
                        for hp in range(2):
                            for r in range(8):
                                if (r % 4) % 2 != hp:
                                    continue
                                cx.op("dve", lambda e, hp=hp, r=r: e.tensor_copy(out=BBC[hp * 64:(hp + 1) * 64, r, 0:nch],
                                                                               in_=PBB[hp * 64:(hp + 1) * 64, r * nch:(r + 1) * nch]),
                                      reads=[B_PBB], pwrites=[B_BBC])
                        for h in range(4):
                            hp = h % 2
                            P0 = hp * 64
                            if hp == 0:
                                cx.dma("sp", LQs[:, 0:S], LQT[h // 2, :, s_off:s_off + S], reads=dtiles("LQT", s_off, S), writes=[B_LQ])
                                cx.dma("sp", LKs[:, 0:S], LKT[h // 2, :, s_off:s_off + S], reads=dtiles("LKT", s_off, S), writes=[B_LK])
                            cx.begin_fill(B_LKh)
                            cx.begin_fill(B_LVh)
                            cx.begin_fill(B_LOh)
                            for c0 in range(0, nch, 8):
                                c1 = min(c0 + 8, nch)
                                rs = slice(s_off + c0 * 128, s_off + c1 * 128)
                                cx.dma("sp", LKh[:, c0:c1, :], LK[rs, h * 64:(h + 1) * 64].rearrange("(c p) d -> p c d", p=128),
                                       reads=dtiles("LK", s_off, S), pwrites=[B_LKh])
                                cx.dma("sp", LVh[:, c0:c1, 0:128], LV[rs, h * 128:(h + 1) * 128].rearrange("(c p) d -> p c d", p=128),
                                       reads=dtiles("LV", s_off, S), pwrites=[B_LVh])
                                cx.dma("sp", LOh[:, c0:c1, :], LO[rs, h * 128:(h + 1) * 128].rearrange("(c p) d -> p c d", p=128),
                                       reads=dtiles("LO", s_off, S), pwrites=[B_LOh])
                            cx.op("dve", lambda e: e.memset(CST[:], 0.0), writes=[B_CSTd[0], B_CSTd[1]])
                            cx.begin_fill(B_HD[0])
                            cx.begin_fill(B_HD[1])
                            NS = 2 * nch

                            def step_info(k):
                                i, d = k // 2, k % 2
                                c = i if d == 0 else nch - 1 - i
                                r = d * 4 + h
                                rc = r if r < 4 else 32 + (r - 4)
                                return c, d, r, rc

                            def stage_a(k):
                                c, d, r, rc = step_info(k)
                                s4, s2 = k % 4, k % 2
                                csl = slice(c * 128, (c + 1) * 128)
                                ecol = ETM[:, c, rc:rc + 1]
                                cx.op("pe", lambda e: e.matmul(PSTb[s2][:, 0:128], lhsT=LKs[P0:P0 + 64, csl],
                                                               rhs=LQs[P0:P0 + 64, csl], start=True, stop=True, skip_group_check=True),
                                      reads=[B_LK, B_LQ], writes=[B_PSTq[s2]])
                                cx.op("act", lambda e: e.activation(out=KKT[s4][:, :], in_=LKh[:, c, :], func=AF.Copy, scale=ecol),
                                      reads=[B_LKh, B_ETM], writes=[B_KKT[s4]])
                                cx.op("pe", lambda e: e.matmul(PUb[s2][P0:P0 + 64, 0:129], lhsT=KKT[s4][:, :],
                                                               rhs=LVh[:, c, :], start=True, stop=True, skip_group_check=True),
                                      reads=[B_KKT[s4], B_LVh], writes=[B_PUq[s2]])

                            def stage_b(k):
                                c, d, r, rc = step_info(k)
                                s4, s2 = k % 4, k % 2
                                csl = slice(c * 128, (c + 1) * 128)
                                bcol = BBC[P0:P0 + 64, r, c:c + 1]
                                ecol = ETM[:, c, rc:rc + 1]
                                cx.op("act", lambda e: e.activation(out=CSBq[s4][P0:P0 + 64, :], in_=CST[P0:P0 + 64, d, :], func=AF.Copy,
                                                                    scale=bcol), reads=[B_CSTd[d], B_BBC], writes=[B_CSBq[s4]])
                                cx.op("dve", lambda e: e.scalar_tensor_tensor(
                                    out=SMT[s4][:, :], in0=PSTb[s2][:, 0:128], scalar=ecol, in1=mask_ap[d], op0=ALU.mult,
                                    op1=ALU.mult), reads=[B_PSTq[s2], B_ETM] + CONSTS, writes=[B_SMT[s4]])

                                def of(e):
                                    e.matmul(POb[s2][:, 0:129], lhsT=LQs[P0:P0 + 64, csl], rhs=CSBq[s4][P0:P0 + 64, :],
                                             start=True, stop=False, skip_group_check=True)
                                    return e.matmul(POb[s2][:, 0:129], lhsT=SMT[s4][:, :], rhs=LVh[:, c, :], start=False,
                                                    stop=True, skip_group_check=True)
                                cx.op("pe", of, reads=[B_LQ, B_CSBq[s4], B_SMT[s4], B_LVh], writes=[B_POq[s2]])
                                cx.op("dve", lambda e: e.scalar_tensor_tensor(
                                    out=CST[P0:P0 + 64, d, :], in0=CST[P0:P0 + 64, d, :], scalar=bcol,
                                    in1=PUb[s2][P0:P0 + 64, 0:129], op0=ALU.mult, op1=ALU.add),
                                    reads=[B_CSTd[d], B_BBC, B_PUq[s2]], writes=[B_CSTd[d]])

                            def stage_c(k):
                                c, d, r, rc = step_info(k)
                                s2 = k % 2
                                if k % 2 == 0:
                                    cx.op("dve", lambda e: e.tensor_copy(out=HD[d][:, c, :], in_=POb[s2][:, 0:129]), reads=[B_POq[s2]],
                                          pwrites=[B_HD[d]])
                                else:
                                    cx.op("act", lambda e: e.copy(out=HD[d][:, c, :], in_=POb[s2][:, 0:129]), reads=[B_POq[s2]],
                                          pwrites=[B_HD[d]])

                            for k in range(NS + 2):
                                if k < NS:
                                    stage_a(k)
                                if 0 <= k - 1 < NS:
                                    stage_b(k - 1)
                                if 0 <= k - 2 < NS:
                                    stage_c(k - 2)
                            for d in range(2):
                                r = d * 4 + h
                                rc = r if r < 4 else 32 + (r - 4)
                                den = HD[d][:, 0:nch, 128]
                                cx.op("dve", lambda e, d=d, den=den: e.tensor_scalar(out=RD[:, d, 0, 0:nch], in0=den, scalar1=-1.0,
                                                                                    scalar2=None, op0=ALU.mult),
                                      reads=[B_HD[d]], writes=[B_RD] if d == 0 else (), pwrites=[B_RD] if d == 1 else ())
                                cx.op("dve", lambda e, d=d, den=den: e.tensor_tensor(out=RD[:, d, 1, 0:nch], in0=RD[:, d, 0, 0:nch], in1=den,
                                                                                    op=ALU.max), reads=[B_HD[d], B_RD], pwrites=[B_RD])
                                cx.op("dve", lambda e, d=d, rc=rc: e.tensor_tensor(out=RD[:, d, 2, 0:nch], in0=RD[:, d, 1, 0:nch],
                                                                                  in1=FTM[:, 0:nch, rc], op=ALU.max),
                                      reads=[B_FTM, B_RD], pwrites=[B_RD])
                                cx.op("dve", lambda e, d=d: e.reciprocal(out=RD[:, d, 3, 0:nch], in_=RD[:, d, 2, 0:nch]), reads=[B_RD],
                                      pwrites=[B_RD])
                            H = HD[0]
                            B_H = B_HD[0]
                            cx.begin_fill(B_SS)
                            for c in range(nch):
                                cx.op("dve", lambda e, c=c: e.tensor_scalar(out=HD[0][:, c, 0:128], in0=HD[0][:, c, 0:128],
                                                                            scalar1=RD[:, 0, 3, c:c + 1], scalar2=None, op0=ALU.mult),
                                      reads=[B_HD[0], B_RD], pwrites=[B_HD[0]])
                                cx.op("dve", lambda e, c=c: e.scalar_tensor_tensor(out=HD[0][:, c, 0:128], in0=HD[1][:, c, 0:128],
                                                                                   scalar=RD[:, 1, 3, c:c + 1], in1=HD[0][:, c, 0:128],
                                                                                   op0=ALU.mult, op1=ALU.add),
                                      reads=[B_HD[0], B_HD[1], B_RD], pwrites=[B_HD[0]])
                                cx.op("act", lambda e, c=c: e.activation(out=JK[:, :], in_=H[:, c, 0:128], func=AF.Square,
                                                                         accum_out=SS[:, 0, c:c + 1]), reads=[B_H], writes=[B_JK],
                                      pwrites=[B_SS])
                            cx.op("dve", lambda e: e.tensor_scalar(out=SS[:, 1, 0:nch], in0=SS[:, 0, 0:nch], scalar1=1.0 / 128.0,
                                                                   scalar2=EPS, op0=ALU.mult, op1=ALU.add), reads=[B_SS], pwrites=[B_SS])
                            cx.op("act", lambda e: e.activation(out=SS[:, 1, 0:nch], in_=SS[:, 1, 0:nch], func=AF.Ln), reads=[B_SS],
                                  pwrites=[B_SS])
                            cx.op("act", lambda e: e.activation(out=SS[:, 2, 0:nch], in_=SS[:, 1, 0:nch], func=AF.Exp, scale=-0.5),
                                  reads=[B_SS], pwrites=[B_SS])
                            cx.op("act", lambda e: e.activation(out=LOh[:, 0:nch, :], in_=LOh[:, 0:nch, :], func=AF.Sigmoid),
                                  reads=[B_LOh], writes=[B_LOh])
                            cx.begin_fill(B_MXL)
                            for c in range(nch):
                                y = c % 2
                                cx.op("dve", lambda e, c=c, y=y: e.scalar_tensor_tensor(
                                    out=T1[y][:, :], in0=H[:, c, 0:128], scalar=SS[:, 2, c:c + 1], in1=GN[:, 1, h * 128:(h + 1) * 128],
                                    op0=ALU.mult, op1=ALU.mult), reads=[B_H, B_SS, B_GN], writes=[B_T1[y]])
                                cx.op("pool", lambda e, c=c, y=y: e.tensor_tensor(out=MXL[:, c, :], in0=T1[y][:, :], in1=LOh[:, c, :],
                                                                                  op=ALU.mult), reads=[B_T1[y], B_LOh], pwrites=[B_MXL])
                            for c0 in range(0, nch, 4):
                                rs = slice(s_off + c0 * 128, s_off + (c0 + 4) * 128)
                                cx.dma("pool", MIX[rs, 512 + h * 128: 512 + (h + 1) * 128].rearrange("(c p) d -> p c d", p=128),
                                       MXL[:, c0:c0 + 4, :], reads=[B_MXL], pwrites=[cx.D("MIX", (s_off + c0 * 128) // 512)])
                    cx.barrier()
                if debug == "mix":
                    break

                def ln_part1(z, bz, st6, mv, bsm):
                    cx.op("dve", lambda e: e.bn_stats(out=st6[:, 0:6], in_=z[:, 0:512]), reads=[bz], writes=[bsm])
                    cx.op("dve", lambda e: e.bn_stats(out=st6[:, 6:12], in_=z[:, 512:1024]), reads=[bz, bsm], pwrites=[bsm])
                    cx.op("dve", lambda e: e.bn_aggr(out=mv[:, 0:2], in_=st6[:, 0:12]), reads=[bsm], pwrites=[bsm])
                    cx.op("dve", lambda e: e.tensor_scalar(out=mv[:, 2:3], in0=mv[:, 1:2], scalar1=EPS, scalar2=None, op0=ALU.add),
                          reads=[bsm], pwrites=[bsm])

                def ln_part2(z, bz, y, by, gi, mv, bsm):
                    cx.op("act", lambda e: e.activation(out=mv[:, 3:4], in_=mv[:, 2:3], func=AF.Ln), reads=[bsm], pwrites=[bsm])
                    cx.op("act", lambda e: e.activation(out=mv[:, 4:5], in_=mv[:, 3:4], func=AF.Exp, scale=-0.5), reads=[bsm], pwrites=[bsm])
                    cx.op("dve", lambda e: e.tensor_scalar(out=y[:, :], in0=z[:, :], scalar1=mv[:, 0:1], scalar2=mv[:, 4:5],
                                                           op0=ALU.subtract, op1=ALU.mult), reads=[bz, bsm], writes=[by])
                    cx.op("dve", lambda e: e.tensor_tensor(out=y[:, :], in0=y[:, :], in1=LNG[:, gi, :], op=ALU.mult),
                          reads=[by, B_LNG], writes=[by])
                    cx.op("pool", lambda e: e.tensor_tensor(out=y[:, :], in0=y[:, :], in1=LNG[:, gi + 1, :], op=ALU.add),
                          reads=[by, B_LNG], writes=[by])

                with ExitStack() as ph:
                    ph.enter_context(nc.named_scope('O%d' % l))
                    Wo = SB(ph, [128, KC, DM], BF16)
                    wst = [SB(ph, [128, KC, 256], F32) for _ in range(2)]
                    mx = [SB(ph, [128, DM], BF16) for _ in range(2)]
                    xr = [SB(ph, [128, DM], F32) for _ in range(2)]
                    mT = [SB(ph, [128, KC, 128], BF16) for _ in range(2)]
                    zt = [SB(ph, [128, DM], F32) for _ in range(2)]
                    yt = [SB(ph, [128, DM], F32) for _ in range(2)]
                    st6 = [SB(ph, [128, 12], F32) for _ in range(2)]
                    mv = [SB(ph, [128, 8], F32) for _ in range(2)]
                    PTB = [PS(ph, [128, 1024], BF16) for _ in range(2)]
                    POx = [[PS(ph, [128, 512]) for _ in range(2)] for _ in range(2)]
                    B_Wo = Buf()
                    B_wst, B_mx, B_xr, B_mT, B_zt, B_yt, B_sm, B_PTB = ([Buf(), Buf()] for _ in range(8))
                    B_POx = [[Buf(), Buf()], [Buf(), Buf()]]
                    cx.begin_fill(B_Wo)
                    for ci, c0 in enumerate(range(0, DM, 256)):
                        s = ci % 2
                        cx.dma("sp", wst[s][:, :, :], w_out_d[l, :, c0:c0 + 256].rearrange("(k p) c -> p k c", p=128), writes=[B_wst[s]])
                        copy_op(("act", "dve", "pool")[ci % 3], Wo[:, :, c0:c0 + 256], wst[s][:, :, :], reads=[B_wst[s]], pwrites=[B_Wo])
                    pend_o = [None]
                    for tt in range(T // 128):
                        s = tt % 2
                        r0 = tt * 128
                        cx.dma("sp", mx[s][:, :], MIX[r0:r0 + 128, :], reads=[cx.D("MIX", r0 // 512)], writes=[B_mx[s]])
                        cx.dma("sp", xr[s][:, :], x_src[r0:r0 + 128, :], reads=[cx.D(x_src_name, r0 // 512)], writes=[B_xr[s]])

                        def trm(e, s=s):
                            last = None
                            for kc in range(KC):
                                last = e.transpose(out=PTB[s][:, kc * 128:(kc + 1) * 128], in_=mx[s][:, kc * 128:(kc + 1) * 128],
                                                   identity=ident_b)
                            return last
                        cx.op("pe", trm, reads=[B_mx[s]] + CONSTS, writes=[B_PTB[s]])
                        copy_op("act", mT[s][:].rearrange("p k c -> p (k c)"), PTB[s][:, :], reads=[B_PTB[s]], writes=[B_mT[s]])
                        for hf in range(2):
                            def mo(e, s=s, hf=hf):
                                last = None
                                for kc in range(KC):
                                    last = e.matmul(POx[s][hf][:, :], lhsT=mT[s][:, kc, :], rhs=Wo[:, kc, hf * 512:(hf + 1) * 512],
                                                    start=(kc == 0), stop=(kc == KC - 1))
                                return last
                            cx.op("pe", mo, reads=[B_mT[s], B_Wo], writes=[B_POx[s][hf]])
                        cx.begin_fill(B_zt[s])
                        for hf in range(2):
                            cx.op("dve", lambda e, s=s, hf=hf: e.scalar_tensor_tensor(
                                out=zt[s][:, hf * 512:(hf + 1) * 512], in0=xr[s][:, hf * 512:(hf + 1) * 512], scalar=float(ALPHA),
                                in1=POx[s][hf][:, :], op0=ALU.mult, op1=ALU.add), reads=[B_xr[s], B_POx[s][hf]], pwrites=[B_zt[s]])
                        ln_part1(zt[s], B_zt[s], st6[s], mv[s], B_sm[s])

                        def fin(s=s, r0=r0):
                            ln_part2(zt[s], B_zt[s], yt[s], B_yt[s], 0, mv[s], B_sm[s])
                            cx.dma("pool", X1[r0:r0 + 128, :], yt[s][:, :], reads=[B_yt[s]], pwrites=[cx.D("X1", r0 // 512)])
                        if pend_o[0] is not None:
                            pend_o[0]()
                        pend_o[0] = fin
                    if pend_o[0] is not None:
                        pend_o[0]()
                    cx.barrier()

                with ExitStack() as ph:
                    ph.enter_context(nc.named_scope('F%d' % l))
                    Wd = SB(ph, [128, NFC, DM], BF16)
                    wst = [SB(ph, [128, 2, DM], F32) for _ in range(2)]
                    wg = [SB(ph, [128, KC, 256], BF16) for _ in range(3)]
                    xr = [SB(ph, [128, 4, DM], F32) for _ in range(2)]
                    hl = [SB(ph, [2, DM], F32) for _ in range(2)]
                    xT = SB(ph, [128, KC, 514], BF16)
                    hm = SB(ph, [128, NFC, 512], BF16)
                    Gt = [SB(ph, [128, 514], F32) for _ in range(2)]
                    cv = [SB(ph, [128, 512], F32) for _ in range(2)]
                    zt = [SB(ph, [128, DM], F32) for _ in range(2)]
                    yt = [SB(ph, [128, DM], F32) for _ in range(2)]
                    st6 = [SB(ph, [128, 12], F32) for _ in range(2)]
                    mv = [SB(ph, [128, 8], F32) for _ in range(2)]
                    PG = [PS(ph, [128, 512]) for _ in range(2)]
                    PUp = [PS(ph, [128, 512]) for _ in range(2)]
                    PH = PS(ph, [128, 512])
                    PD = [PS(ph, [128, 512]) for _ in range(2)]
                    PT = PS(ph, [128, 512])
                    B_Wd, B_xT, B_hm, B_PH, B_PT = Buf(), Buf(), Buf(), Buf(), Buf()
                    B_wst, B_xr, B_hl, B_Gt, B_cv, B_zt, B_yt, B_sm, B_PG, B_PUp, B_PD = ([Buf(), Buf()] for _ in range(11))
                    B_wg = [Buf(), Buf(), Buf()]
                    cx.begin_fill(B_Wd)
                    for ci in range(NFC // 2):
                        s = ci % 2
                        cx.dma("sp", wst[s][:, :, :], w_down_d[l, ci * 256:(ci + 1) * 256, :].rearrange("(c p) d -> p c d", p=128),
                               writes=[B_wst[s]])
                        copy_op(("act", "dve", "pool")[ci % 3], Wd[:, 2 * ci:2 * ci + 2, :], wst[s][:, :, :], reads=[B_wst[s]],
                                pwrites=[B_Wd])
                    gi_ = [0]
                    tiles = []
                    for (s_off, S) in seqs:
                        for t0 in range(s_off, s_off + S, 512):
                            tiles.append((t0, t0 == s_off, t0 + 512 == s_off + S))
                    wq = [(ti, fc) for ti in range(len(tiles)) for fc in range(NFC)]
                    wpos = [0]

                    def prefetch():
                        if wpos[0] < len(wq):
                            _, fc = wq[wpos[0]]
                            s3 = wpos[0] % 3
                            cx.dma("sp", wg[s3][:].rearrange("p k c -> p (k c)"), WGUd[l, fc], reads=[cx.D("WGU", (l, fc))],
                                   writes=[B_wg[s3]])
                            wpos[0] += 1
                    prefetch()
                    prefetch()
                    wi = 0
                    pend_f = [None]
                    for tix, (t0, first, last) in enumerate(tiles):
                        s = tix % 2
                        d1 = [cx.D("X1", t0 // 512)]
                        cx.dma("sp", xr[s][:], X1[t0:t0 + 512, :].rearrange("(a p) d -> p a d", p=128), reads=d1, writes=[B_xr[s]])
                        cx.begin_fill(B_hl[s])
                        if not first:
                            cx.dma("sp", hl[s][0:1, :], X1[t0 - 1:t0, :], reads=[cx.D("X1", (t0 - 1) // 512)], pwrites=[B_hl[s]])
                        if not last:
                            cx.dma("sp", hl[s][1:2, :], X1[t0 + 512:t0 + 513, :], reads=[cx.D("X1", (t0 + 512) // 512)], pwrites=[B_hl[s]])
                        cx.begin_fill(B_xT)
                        for kc in range(KC):
                            def tr(e, kc=kc, s=s):
                                last_ = None
                                for a in range(4):
                                    last_ = e.transpose(out=PT[:, a * 128:(a + 1) * 128], in_=xr[s][:, a, kc * 128:(kc + 1) * 128],
                                                        identity=ident_f)
                                return last_
                            cx.op("pe", tr, reads=[B_xr[s]] + CONSTS, writes=[B_PT])
                            copy_op(evac_eng(), xT[:, kc, 1:513], PT[:, :], reads=[B_PT], pwrites=[B_xT])
                        if first and last:
                            pass
                        if not (first and last):
                            def trh(e, s=s):
                                last_ = None
                                for kc in range(KC):
                                    last_ = e.transpose(out=PT[:, kc * 2:(kc + 1) * 2], in_=hl[s][0:2, kc * 128:(kc + 1) * 128],
                                                        identity=ident_f[0:2, 0:2])
                                return last_
                            cx.op("pe", trh, reads=[B_hl[s]] + CONSTS, writes=[B_PT])
                            cx.op("dve", lambda e: e.tensor_copy(out=xT[:, :, 0:514:513], in_=PT[:, 0:16].rearrange("p (k t) -> p k t", t=2)),
                                  reads=[B_PT], pwrites=[B_xT])
                        cx.begin_fill(B_hm)
                        for fc in range(NFC):
                            s3 = wi % 3
                            wi += 1
                            prefetch()
                            g2 = fc % 2
                            W = wg[s3]

                            def gm(e, W=W, g2=g2):
                                last_ = None
                                for kc in range(KC):
                                    last_ = e.matmul(PG[g2][:, :], lhsT=W[:, kc, 0:128], rhs=xT[:, kc, 1:513], start=(kc == 0), stop=(kc == KC - 1))
                                return last_
                            cx.op("pe", gm, reads=[B_wg[s3], B_xT], writes=[B_PG[g2]])
                            if not (first and last):
                                def gh(e, W=W, fc=fc):
                                    last_ = None
                                    for kc in range(KC):
                                        last_ = e.matmul(PH[:, fc * 2:fc * 2 + 2], lhsT=W[:, kc, 0:128], rhs=xT[:, kc, 0:514:513], start=(kc == 0),
                                                         stop=(kc == KC - 1), skip_group_check=True)
                                    return last_
                                cx.op("pe", gh, reads=[B_wg[s3], B_xT], writes=[B_PH])

                            def um(e, W=W, g2=g2):
                                last_ = None
                                for kc in range(KC):
                                    last_ = e.matmul(PUp[g2][:, :], lhsT=W[:, kc, 128:256], rhs=xT[:, kc, 1:513], start=(kc == 0), stop=(kc == KC - 1))
                                return last_
                            cx.op("pe", um, reads=[B_wg[s3], B_xT], writes=[B_PUp[g2]])
                            G = Gt[g2]
                            cx.begin_fill(B_Gt[g2])
                            cx.op("act", lambda e, G=G, g2=g2: e.copy(out=G[:, 1:513], in_=PG[g2][:, :]), reads=[B_PG[g2]], pwrites=[B_Gt[g2]])
                            if not (first and last):
                                cx.op("dve", lambda e, G=G, fc=fc: e.tensor_copy(out=G[:, 0:514:513], in_=PH[:, fc * 2:fc * 2 + 2]),
                                      reads=[B_PH], pwrites=[B_Gt[g2]])
                            if first:
                                cx.op("dve", lambda e, G=G: e.memset(G[:, 0:1], 0.0), reads=[B_Gt[g2]], pwrites=[B_Gt[g2]])
                            if last:
                                cx.op("dve", lambda e, G=G: e.memset(G[:, 513:514], 0.0), reads=[B_Gt[g2]], pwrites=[B_Gt[g2]])
                            C = cv[g2]
                            cx.op("dve", lambda e, G=G, C=C, fc=fc: e.tensor_scalar(out=C[:, :], in0=G[:, 1:513], scalar1=CW[:, fc, 1:2],
                                                                                  scalar2=CW[:, fc, 3:4], op0=ALU.mult, op1=ALU.add),
                                  reads=[B_Gt[g2], B_CW], writes=[B_cv[g2]])
                            cx.op("dve", lambda e, G=G, C=C, fc=fc: e.scalar_tensor_tensor(out=C[:, :], in0=G[:, 0:512], scalar=CW[:, fc, 0:1],
                                                                                         in1=C[:, :], op0=ALU.mult, op1=ALU.add),
                                  reads=[B_Gt[g2], B_CW, B_cv[g2]], writes=[B_cv[g2]])
                            cx.op("dve", lambda e, G=G, C=C, fc=fc: e.scalar_tensor_tensor(out=C[:, :], in0=G[:, 2:514], scalar=CW[:, fc, 2:3],
                                                                                         in1=C[:, :], op0=ALU.mult, op1=ALU.add),
                                  reads=[B_Gt[g2], B_CW, B_cv[g2]], writes=[B_cv[g2]])
                            cx.op("act", lambda e, C=C: e.activation(out=C[:, :], in_=C[:, :], func=AF.Gelu), reads=[B_cv[g2]], writes=[B_cv[g2]])
                            cx.op("dve", lambda e, C=C, fc=fc, g2=g2: e.tensor_tensor(out=hm[:, fc, :], in0=C[:, :], in1=PUp[g2][:, :], op=ALU.mult),
                                  reads=[B_cv[g2], B_PUp[g2]], pwrites=[B_hm])
                        for a in range(4):
                            z2 = a % 2
                            for hf in range(2):
                                def dm(e, a=a, hf=hf):
                                    last_ = None
                                    for fc in range(NFC):
                                        last_ = e.matmul(PD[hf][:, :], lhsT=hm[:, fc, a * 128:(a + 1) * 128], rhs=Wd[:, fc, hf * 512:(hf + 1) * 512],
                                                         start=(fc == 0), stop=(fc == NFC - 1))
                                    return last_
                                cx.op("pe", dm, reads=[B_hm, B_Wd], writes=[B_PD[hf]])
                            cx.begin_fill(B_zt[z2])
                            for hf in range(2):
                                cx.op("dve", lambda e, a=a, hf=hf, z2=z2, s=s: e.scalar_tensor_tensor(
                                    out=zt[z2][:, hf * 512:(hf + 1) * 512], in0=xr[s][:, a, hf * 512:(hf + 1) * 512], scalar=float(ALPHA),
                                    in1=PD[hf][:, :], op0=ALU.mult, op1=ALU.add), reads=[B_xr[s], B_PD[hf]], pwrites=[B_zt[z2]])
                            ln_part1(zt[z2], B_zt[z2], st6[z2], mv[z2], B_sm[z2])

                            def finf(z2=z2, r0=t0 + a * 128, t0=t0):
                                ln_part2(zt[z2], B_zt[z2], yt[z2], B_yt[z2], 2, mv[z2], B_sm[z2])
                                cx.dma("pool", y_dst[r0:r0 + 128, :], yt[z2][:, :], reads=[B_yt[z2]], pwrites=[cx.D(y_dst_name, t0 // 512)])
                            if pend_f[0] is not None:
                                pend_f[0]()
                            pend_f[0] = finf
                    if pend_f[0] is not None:
                        pend_f[0]()
                    cx.barrier()
            cx.barrier()
            for e in ("sp",):
                pass

        block.sync(main)
    return nc


SEQ_LENS = [4096, 4096, 2048, 2048, 2048, 2048]
_WNAMES = ["w_in", "gate_bias", "lam_q1", "lam_k1", "lam_q2", "lam_k2", "att_norm_g", "lstm_norm_g", "w_out",
           "ln1_g", "ln1_b", "w_gu", "conv_w", "conv_b", "w_down", "ln2_g", "ln2_b"]


def kernel(x_prompt, x_sample, **w):
    x_prompt = np.asarray(x_prompt, np.float32)
    x_sample = np.asarray(x_sample, np.float32)
    n = 8
    C, CBG = _build_consts()
    wd = {k: np.ascontiguousarray(np.asarray(w[k], np.float32)) for k in _WNAMES}
    in_maps = []
    for i in range(n):
        xs = np.concatenate([x_prompt[2 * i:2 * i + 2].reshape(-1, DM), x_sample[4 * i:4 * i + 4].reshape(-1, DM)], axis=0)
        m = dict(wd)
        m["x"] = np.ascontiguousarray(xs)
        m["consts"] = C
        m["constsb"] = CBG
        in_maps.append(m)
    nc = build(SEQ_LENS)
    res = run_bass_kernel_spmd(nc, in_maps, core_ids=list(range(n)))
    yp = np.empty_like(x_prompt)
    ys = np.empty_like(x_sample)
    for i in range(n):
        y = np.asarray(res.results[i]["y"], np.float32)
        yp[2 * i:2 * i + 2] = y[0:8192].reshape(2, 4096, DM)
        ys[4 * i:4 * i + 4] = y[8192:16384].reshape(4, 2048, DM)
    return (yp, ys)
```

```python
import math
from contextlib import ExitStack
import numpy as np
import concourse.bass as bass
import concourse.mybir as mybir
from concourse.bass_utils import run_bass_kernel_spmd

F32, BF16, F16 = mybir.dt.float32, mybir.dt.bfloat16, mybir.dt.float16
ALU, AF, AX = mybir.AluOpType, mybir.ActivationFunctionType, mybir.AxisListType

DM = 1024
KC = 8
INW = 3088
DFF = 2816
NFC = 22
DEPTH = 2
ALPHA = (2 * DEPTH) ** 0.25
EPS = 1e-5
SLOPES = [2.0 ** (-8.0 * (h + 1) / 4) for h in range(4)]


def _const_layout():
    off = {}
    c = 0
    for name, w in (("ident", 128), ("bb", 128), ("ba", 128),
                    ("fb", 16), ("fa", 16), ("mf", 128), ("mb", 128), ("sel", 512), ("ones", 128)):
        off[name] = (c, w)
        c += w
    return off, c


BIGOFF = {"dA": (0, 2048), "dB": (2048, 2048), "NI": (4096, 512)}
NBIG = 4608


def _build_consts():
    off, n = _const_layout()
    C = np.zeros((128, n), np.float32)
    CBG = np.zeros((128, NBIG), np.float32)
    j = np.arange(128)[:, None].astype(np.float64)
    C[:, off["ident"][0]:off["ident"][0] + 128] = np.eye(128)
    i = np.arange(512)[None, :].astype(np.float64)
    for d in range(4):
        dist = np.abs(i - 128 * d - j)
        CBG[:, d * 512:(d + 1) * 512] = np.minimum(dist, 256)
        CBG[:, 2048 + d * 512: 2048 + (d + 1) * 512] = np.maximum(dist - 256, 0)
    for h in range(4):
        sl = SLOPES[h]
        CBG[:, 4096 + h * 128: 4096 + (h + 1) * 128] = -sl * np.eye(128)
        for D in range(32):
            C[:, off["bb"][0] + h * 32 + D] = (sl * (j - 128 * D))[:, 0]
            C[:, off["ba"][0] + h * 32 + D] = (-sl * (128 * D + j - 511))[:, 0]
        for k in range(4):
            C[:, off["fb"][0] + h * 4 + k] = np.exp(-sl * (128 * k + j))[:, 0]
            C[:, off["fa"][0] + h * 4 + k] = np.exp(-sl * (511 - 128 * k - j))[:, 0]
    jj = np.arange(128)[None, :]
    C[:, off["mf"][0]:off["mf"][0] + 128] = (j <= jj)
    C[:, off["mb"][0]:off["mb"][0] + 128] = (j >= jj)
    for r in range(8):
        p = r if r < 4 else 32 + (r - 4)
        C[p, off["sel"][0] + r * 64: off["sel"][0] + (r + 1) * 64] = 1.0
    C[:, off["ones"][0]:off["ones"][0] + 128] = 1.0
    return C, CBG


class Buf:
    __slots__ = ("w", "r", "old")

    def __init__(self):
        self.w = []
        self.r = {}
        self.old = []


class _Rec:
    def __init__(self, eng):
        self._eng = eng
        self.first = None

    def __getattr__(self, name):
        f = getattr(self._eng, name)

        def w(*a, **k):
            r = f(*a, **k)
            if self.first is None:
                self.first = r
            return r
        return w


class Ctx:
    def __init__(self, nc, es):
        self.nc = nc
        self.eng = {"pe": nc.tensor, "act": nc.scalar, "dve": nc.vector, "pool": nc.gpsimd, "sp": nc.sync}
        self.sem = {k: es.enter_context(nc.semaphore("s_" + k)) for k in self.eng}
        self.cnt = {k: 0 for k in self.eng}
        self.seen = {k: {} for k in self.eng}
        self.dq = {}
        for q in ("sp", "pool"):
            self.dq[q] = [[es.enter_context(nc.semaphore("d_%s%d" % (q, i))), 0, "d_%s%d" % (q, i)] for i in range(8)]
        self.dq_i = {"sp": 0, "pool": 0}
        self.dbufs = {}
        self.n_inst = 0

    def D(self, name, idx):
        k = (name, idx)
        b = self.dbufs.get(k)
        if b is None:
            b = self.dbufs[k] = Buf()
        return b

    def _wait(self, e, evs, defer=False):
        seen = self.seen[e]
        best = {}
        for (sem, key, val) in evs:
            if seen.get(key, 0) >= val:
                continue
            if key not in best or best[key][1] < val:
                best[key] = (sem, val)
        items = list(best.items())
        deferred = None
        if defer and items:
            key, (sem, val) = items.pop()
            deferred = (sem, val)
            seen[key] = val
        for key, (sem, val) in items:
            self.eng[e].wait_ge(sem, val)
            seen[key] = val
        return deferred

    def _deps(self, e, reads, writes, pwrites):
        evs = []
        for b in reads:
            evs.extend(b.w)
        for b in writes:
            for ev in b.w:
                if ev[1] != e:
                    evs.append(ev)
            for key, ev in b.r.items():
                if key != e:
                    evs.append(ev)
            for ev in b.old:
                if ev[1] != e:
                    evs.append(ev)
        for b in pwrites:
            for ev in b.old:
                if ev[1] != e:
                    evs.append(ev)
        if e == "pe":
            evs = [ev for ev in evs if ev[1] != "pe"]
        return evs

    def _update(self, ev, key, reads, writes, pwrites):
        for b in reads:
            b.r[key] = ev
        for b in writes:
            b.w = [ev]
            b.r = {}
            b.old = []
        for b in pwrites:
            b.w.append(ev)

    def begin_fill(self, b):
        b.old = list(b.r.values()) + list(b.w)
        b.w = []
        b.r = {}

    def op(self, e, fn, reads=(), writes=(), pwrites=()):
        deferred = self._wait(e, self._deps(e, reads, writes, pwrites), defer=True)
        if deferred is None:
            inst = fn(self.eng[e])
        else:
            rec = _Rec(self.eng[e])
            inst = fn(rec)
            rec.first._wait_ge(deferred[0], deferred[1])
        self.cnt[e] += 1
        inst.then_inc(self.sem[e], 1)
        ev = (self.sem[e], e, self.cnt[e])
        self._update(ev, e, reads, writes, pwrites)
        self.n_inst += 1
        return ev

    def dma(self, q, out, in_, reads=(), writes=(), pwrites=()):
        evs = self._deps("__dma__", reads, writes, pwrites)
        slot = self.dq[q][self.dq_i[q] % 8]
        self.dq_i[q] += 1
        if slot[1] > 0:
            evs = evs + [(slot[0], slot[2], slot[1])]
        deferred = self._wait(q, evs, defer=True)
        di = self.eng[q].dma_start(out=out, in_=in_)
        if deferred is not None:
            di._wait_ge(deferred[0], deferred[1])
        di.then_inc(slot[0], 16)
        slot[1] += 16
        ev = (slot[0], slot[2], slot[1])
        self._update(ev, slot[2] + "_%d" % slot[1], reads, writes, pwrites)
        self.n_inst += 1
        return ev

    def barrier(self):
        evs = [(self.sem[k], k, self.cnt[k]) for k in self.eng if self.cnt[k] > 0]
        for q in self.dq:
            for slot in self.dq[q]:
                if slot[1] > 0:
                    evs.append((slot[0], slot[2], slot[1]))
        for e in self.eng:
            self._wait(e, [ev for ev in evs if ev[1] != e])


def build(seq_lens, n_layers=DEPTH, debug=None):
    T = sum(seq_lens)
    SMAX = max(seq_lens)
    NCH_MAX = SMAX // 128
    seqs = []
    o = 0
    for S in seq_lens:
        assert S % 512 == 0
        seqs.append((o, S))
        o += S
    coff, NCONST = _const_layout()

    nc = bass.Bass("TRN2", target_bir_lowering=False)

    def din(name, shape, dt=F32):
        return nc.dram_tensor(name, shape, dt, kind="ExternalInput").ap()

    def dscr(name, shape, dt):
        return nc.dram_tensor(name, shape, dt, kind="Internal").ap()

    x_in = din("x", [T, DM])
    consts_d = din("consts", [128, NCONST])
    constsb_d = din("constsb", [128, NBIG])
    w_in_d = din("w_in", [DEPTH, DM, INW])
    gate_bias_d = din("gate_bias", [DEPTH, 16])
    lam_d = [din(n, [DEPTH, 64]) for n in ("lam_q1", "lam_k1", "lam_q2", "lam_k2")]
    att_g_d = din("att_norm_g", [DEPTH, 512])
    lstm_g_d = din("lstm_norm_g", [DEPTH, 512])
    w_out_d = din("w_out", [DEPTH, DM, DM])
    ln1_g_d = din("ln1_g", [DEPTH, DM])
    ln1_b_d = din("ln1_b", [DEPTH, DM])
    w_gu_d = din("w_gu", [DEPTH, DM, 2 * DFF])
    conv_w_d = din("conv_w", [DEPTH, 3, DFF])
    conv_b_d = din("conv_b", [DEPTH, DFF])
    w_down_d = din("w_down", [DEPTH, DFF, DM])
    ln2_g_d = din("ln2_g", [DEPTH, DM])
    ln2_b_d = din("ln2_b", [DEPTH, DM])
    y_out = nc.dram_tensor("y", [T, DM], F32, kind="ExternalOutput").ap()

    QT = dscr("QT", [4, 128, T], BF16)
    KT = dscr("KT", [4, 128, T], BF16)
    VA = dscr("VA", [T, 512], BF16)
    LQT = dscr("LQT", [2, 128, T], BF16)
    LKT = dscr("LKT", [2, 128, T], BF16)
    LK = dscr("LK", [T, 256], BF16)
    LV = dscr("LV", [T, 512], BF16)
    LO = dscr("LO", [T, 512], F32)
    IGd = dscr("IGd", [8, T], F32)
    FGd = dscr("FGd", [8, T], F32)
    MIX = (nc.dram_tensor("MIX", [T, DM], BF16, kind="ExternalOutput").ap() if debug else dscr("MIX", [T, DM], BF16))
    X1 = dscr("X1", [T, DM], F32)
    X2 = dscr("X2", [T, DM], F32)
    WGUd = dscr("WGUd", [DEPTH, NFC, 128, 8 * 256], BF16)

    uid = [0]

    with ExitStack() as es:
        cx = Ctx(nc, es)

        def SB(st, shape, dt):
            uid[0] += 1
            return st.enter_context(nc.sbuf_tensor("sb%d" % uid[0], shape, dt))

        def PS(st, shape, dt=F32):
            uid[0] += 1
            return st.enter_context(nc.psum_tensor("ps%d" % uid[0], shape, dt))

        block = es.enter_context(nc.Block())

        def main(_e):
            CF = SB(es, [128, NCONST], F32)
            B_CF = Buf()
            cx.dma("sp", CF[:], consts_d[:, :], writes=[B_CF])

            def cf(name, a=0, b=None):
                c0, w = coff[name]
                return CF[:, c0 + a: c0 + (w if b is None else b)]

            ident_f = cf("ident")
            CB = SB(es, [128, 512 + 128 + 128], BF16)
            CH = SB(es, [128, 2048 + 512], F16)
            B_CH = Buf()
            B_CB = Buf()
            cx.begin_fill(B_CB)
            with ExitStack() as ph:
                CBF = SB(ph, [128, NBIG], F32)
                B_CBF = Buf()
                cx.dma("sp", CBF[:], constsb_d[:, :], writes=[B_CBF])
                cx.op("dve", lambda e: e.tensor_tensor(out=CH[:, 0:2048], in0=CBF[:, 0:2048], in1=CBF[:, 2048:4096], op=ALU.add),
                      reads=[B_CBF], writes=[B_CH])
                cx.op("dve", lambda e: e.tensor_copy(out=CH[:, 2048:2560], in_=CBF[:, 4096:4608]), reads=[B_CBF, B_CH], pwrites=[B_CH])
                cx.op("dve", lambda e: e.tensor_copy(out=CB[:, 0:512], in_=CBF[:, 4096:4608]), reads=[B_CBF], pwrites=[B_CB])
                cx.op("dve", lambda e: e.tensor_copy(out=CB[:, 512:640], in_=cf("ones")), reads=[B_CF], pwrites=[B_CB])
                cx.op("dve", lambda e: e.tensor_copy(out=CB[:, 640:768], in_=cf("ident")), reads=[B_CF], pwrites=[B_CB])
                cx.barrier()
            NI_b = lambda h: CB[:, h * 128:(h + 1) * 128]
            ones_b = CB[:, 512:640]
            ident_b = CB[:, 640:768]
            CONSTS = [B_CF, B_CB, B_CH]
            D16 = lambda d: CH[:, d * 512:(d + 1) * 512]
            NI16 = lambda h: CH[:, 2048 + h * 128: 2048 + (h + 1) * 128]

            LNG = SB(es, [128, 4, DM], F32)
            GN = SB(es, [128, 2, 512], F32)
            LAMV = SB(es, [128, 4, 64], F32)
            LAMC = SB(es, [128, 8], F32)
            GBC = SB(es, [64, 2], F32)
            CW = SB(es, [128, NFC, 4], F32)
            B_LNG, B_GN, B_LAM, B_GBC, B_CWR, B_CW = Buf(), Buf(), Buf(), Buf(), Buf(), Buf()
            NT512 = T // 512
            NM = SB(es, [128, 8, NT512], F32)
            BLK = SB(es, [128, 128], BF16)
            HSEL = SB(es, [128, 2, 128], F32)
            B_BLK = Buf()
            cx.op('pool', lambda e: e.memset(BLK[:], 0.0), writes=[B_BLK])
            cx.op('pool', lambda e: e.memset(BLK[0:64, 0:64], 1.0), reads=[B_BLK], pwrites=[B_BLK])
            cx.op('pool', lambda e: e.memset(BLK[64:128, 64:128], 1.0), reads=[B_BLK], pwrites=[B_BLK])
            cx.op('pool', lambda e: e.memset(HSEL[:], 0.0), reads=[B_BLK], pwrites=[B_BLK])
            cx.op('pool', lambda e: e.memset(HSEL[0:64, 0, :], 1.0 / 64.0), reads=[B_BLK], pwrites=[B_BLK])
            cx.op('pool', lambda e: e.memset(HSEL[64:128, 1, :], 1.0 / 64.0), reads=[B_BLK], pwrites=[B_BLK])
            B_NM = Buf()

            rr = [0]

            def evac_eng():
                rr[0] += 1
                return "act" if rr[0] % 2 else "dve"

            def copy_op(e, out, in_, reads, writes=(), pwrites=(), scale=None):
                if e == "act":
                    if scale is None:
                        return cx.op("act", lambda g: g.copy(out=out, in_=in_), reads=reads, writes=writes, pwrites=pwrites)
                    return cx.op("act", lambda g: g.mul(out=out, in_=in_, mul=scale), reads=reads, writes=writes, pwrites=pwrites)
                if scale is None:
                    return cx.op(e, lambda g: g.tensor_copy(out=out, in_=in_), reads=reads, writes=writes, pwrites=pwrites)
                return cx.op(e, lambda g: g.tensor_scalar(out=out, in0=in_, scalar1=float(scale), scalar2=None, op0=ALU.mult),
                             reads=reads, writes=writes, pwrites=pwrites)

            def wgu_convert(ph):
                stf = [SB(ph, [128, 8, 256], F32) for _ in range(2)]
                stb_ = [SB(ph, [128, 8, 256], BF16) for _ in range(2)]
                B_f = [Buf(), Buf()]
                B_b = [Buf(), Buf()]
                it = 0
                for l_ in range(n_layers):
                    for fc in range(NFC):
                        s_ = it % 2
                        it += 1
                        cx.begin_fill(B_f[s_])
                        cx.dma("sp", stf[s_][:, :, 0:128],
                               w_gu_d[l_, :, fc * 128:(fc + 1) * 128].rearrange("(k p) c -> p k c", p=128), pwrites=[B_f[s_]])
                        cx.dma("sp", stf[s_][:, :, 128:256],
                               w_gu_d[l_, :, DFF + fc * 128: DFF + (fc + 1) * 128].rearrange("(k p) c -> p k c", p=128),
                               pwrites=[B_f[s_]])
                        yield
                        copy_op(evac_eng(), stb_[s_][:], stf[s_][:], reads=[B_f[s_]], writes=[B_b[s_]])
                        cx.dma("pool", WGUd[l_, fc], stb_[s_][:].rearrange("p k c -> p (k c)"), reads=[B_b[s_]],
                               writes=[cx.D("WGU", (l_, fc))])
                        yield

            for l in range(n_layers):
                lam_init = 0.8 - 0.6 * math.exp(-0.3 * l)
                x_src = x_in if l == 0 else X2
                x_src_name = "xin" if l == 0 else "X2"
                y_dst = y_out if l == n_layers - 1 else X2
                y_dst_name = "yout" if l == n_layers - 1 else "X2"

                cx.begin_fill(B_LNG)
                for i, d in enumerate((ln1_g_d, ln1_b_d, ln2_g_d, ln2_b_d)):
                    cx.dma("sp", LNG[:, i, :], d[l, :].partition_broadcast(128), pwrites=[B_LNG])
                cx.begin_fill(B_GN)
                cx.dma("sp", GN[:, 0, :], att_g_d[l, :].partition_broadcast(128), pwrites=[B_GN])
                cx.dma("sp", GN[:, 1, :], lstm_g_d[l, :].partition_broadcast(128), pwrites=[B_GN])
                cx.op("dve", lambda e: e.tensor_scalar(out=GN[:, 0, :], in0=GN[:, 0, :], scalar1=float(1.0 - lam_init),
                                                       scalar2=None, op0=ALU.mult), reads=[B_GN], pwrites=[B_GN])
                cx.begin_fill(B_LAM)
                for i in range(4):
                    cx.dma("sp", LAMV[:, i, :], lam_d[i][l, :].partition_broadcast(128), pwrites=[B_LAM])
                B_LAMC = Buf()
                cx.op("dve", lambda e: e.tensor_tensor(out=LAMV[:, 0, :], in0=LAMV[:, 0, :], in1=LAMV[:, 1, :], op=ALU.mult),
                      reads=[B_LAM], pwrites=[B_LAM])
                cx.op("dve", lambda e: e.tensor_tensor(out=LAMV[:, 2, :], in0=LAMV[:, 2, :], in1=LAMV[:, 3, :], op=ALU.mult),
                      reads=[B_LAM], pwrites=[B_LAM])
                cx.op("dve", lambda e: e.tensor_reduce(out=LAMC[:, 1:2], in_=LAMV[:, 0, :], axis=AX.X, op=ALU.add),
                      reads=[B_LAM], writes=[B_LAMC])
                cx.op("dve", lambda e: e.tensor_reduce(out=LAMC[:, 2:3], in_=LAMV[:, 2, :], axis=AX.X, op=ALU.add),
                      reads=[B_LAMC, B_LAM], pwrites=[B_LAMC])
                cx.op("act", lambda e: e.activation(out=LAMC[:, 3:5], in_=LAMC[:, 1:3], func=AF.Exp), reads=[B_LAMC],
                      pwrites=[B_LAMC])
                cx.op("dve", lambda e: e.tensor_tensor(out=LAMC[:, 5:6], in0=LAMC[:, 4:5], in1=LAMC[:, 3:4], op=ALU.subtract),
                      reads=[B_LAMC], pwrites=[B_LAMC])
                cx.op("dve", lambda e: e.tensor_scalar(out=LAMC[:, 0:1], in0=LAMC[:, 5:6], scalar1=float(-lam_init),
                                                       scalar2=None, op0=ALU.add), reads=[B_LAMC], pwrites=[B_LAMC])
                NEGLAM = LAMC[:, 0:1]
                cx.op("pool", lambda e: e.memset(GBC[:], 0.0), writes=[B_GBC])
                for (c, r0, g0) in ((0, 0, 0), (0, 32, 8), (1, 0, 4), (1, 32, 12)):
                    cx.dma("sp", GBC[r0:r0 + 4, c:c + 1], gate_bias_d[l, g0:g0 + 4].rearrange("(p o) -> p o", o=1),
                           reads=[B_GBC], pwrites=[B_GBC])
                with ExitStack() as ph:
                    CWR = SB(ph, [4, DFF], F32)
                    B_CWR = Buf()
                    cx.begin_fill(B_CWR)
                    cx.dma("sp", CWR[0:3, :], conv_w_d[l, :, :], pwrites=[B_CWR])
                    cx.dma("sp", CWR[3:4, :], conv_b_d[l:l + 1, :], pwrites=[B_CWR])
                    pcw = PS(ph, [128, 512])
                    B_pcw = Buf()

                    def cw_t(e):
                        last = None
                        for fc in range(NFC):
                            last = e.transpose(out=pcw[:, fc * 4:(fc + 1) * 4], in_=CWR[0:4, fc * 128:(fc + 1) * 128],
                                               identity=ident_f[0:4, 0:4])
                        return last
                    cx.op("pe", cw_t, reads=[B_CWR] + CONSTS, writes=[B_pcw])
                    cx.op("dve", lambda e: e.tensor_copy(out=CW[:].rearrange("p f c -> p (f c)"), in_=pcw[:, 0:NFC * 4]),
                          reads=[B_pcw], writes=[B_CW])
                    cx.barrier()

                with ExitStack() as ph:
                    ph.enter_context(nc.named_scope('P%d' % l))
                    Win = SB(ph, [128, KC, INW], BF16)
                    WG = SB(ph, [128, KC, 2, 36], BF16)
                    wst = [SB(ph, [128, KC, 256], F32) for _ in range(2)]
                    xs = [SB(ph, [128, 4, DM], F32) for _ in range(2)]
                    xT = [SB(ph, [128, KC, 512], BF16) for _ in range(2)]
                    stb = [SB(ph, [128, 512], BF16) for _ in range(6)]
                    stf = [SB(ph, [128, 512], F32) for _ in range(4)]
                    ptr = [PS(ph, [128, 512]) for _ in range(2)]
                    pmm = [PS(ph, [128, 512]) for _ in range(4)]
                    PN = [PS(ph, [128, 512]) for _ in range(2)]
                    SQ = [SB(ph, [128, 512], BF16) for _ in range(2)]
                    B_PN, B_SQ = [Buf(), Buf()], [Buf(), Buf()]
                    cx.begin_fill(B_NM)
                    B_Win, B_WG = Buf(), Buf()
                    B_wst = [Buf(), Buf()]
                    B_xs = [Buf(), Buf()]
                    B_xT = [Buf(), Buf()]
                    B_stb = [Buf() for _ in range(6)]
                    B_stf = [Buf() for _ in range(4)]
                    B_ptr = [Buf(), Buf()]
                    B_pmm = [Buf() for _ in range(4)]
                    cx.begin_fill(B_Win)
                    for ci, c0 in enumerate(range(0, INW, 256)):
                        w = min(256, INW - c0)
                        s = ci % 2
                        cx.dma("sp", wst[s][:, :, 0:w], w_in_d[l, :, c0:c0 + w].rearrange("(k p) c -> p k c", p=128),
                               writes=[B_wst[s]])
                        copy_op(("act", "dve", "pool")[ci % 3], Win[:, :, c0:c0 + w], wst[s][:, :, 0:w], reads=[B_wst[s]],
                                pwrites=[B_Win])
                    cx.op("pool", lambda e: e.memset(WG[:], 0.0), writes=[B_WG])
                    for (g, r0, c0) in ((0, 0, 0), (0, 32, 8), (1, 0, 4), (1, 32, 12)):
                        cx.op("dve", lambda e, g=g, r0=r0, c0=c0: e.tensor_copy(
                            out=WG[:, :, g, r0:r0 + 4], in_=Win[:, :, 3072 + c0:3072 + c0 + 4]),
                            reads=[B_Win, B_WG], pwrites=[B_WG])
                    ib, jf, im = [0], [0], [0]

                    def mm_group(out_ap, lhs_fn, rhs_fn, reads):
                        i = im[0] % 4
                        im[0] += 1
                        pt = pmm[i]

                        def f(e):
                            last = None
                            for kc in range(KC):
                                last = e.matmul(out_ap(pt), lhsT=lhs_fn(kc), rhs=rhs_fn(kc), start=(kc == 0), stop=(kc == KC - 1))
                            return last
                        cx.op("pe", f, reads=reads, writes=[B_pmm[i]])
                        return pt, B_pmm[i]

                    wgen = wgu_convert(ph) if l == 0 else None
                    pend = []

                    def norm_item(fi, i, ti):
                        z = fi % 2
                        cx.op("act", lambda e: e.activation(out=SQ[z][:, :], in_=stb[i][:, :], func=AF.Square),
                              reads=[B_stb[i]], writes=[B_SQ[z]])

                        cx.op("pe", lambda e: e.matmul(PN[z][:, :], lhsT=BLK[:, :], rhs=SQ[z][:, :], start=True, stop=True),
                              reads=[B_SQ[z], B_BLK], writes=[B_PN[z]])
                        qk, hh = fi // 4, fi % 4
                        col = qk * 4 + hh
                        cx.op("dve", lambda e: e.tensor_reduce(out=NM[:, col, ti:ti + 1], in_=PN[z][:, :], axis=AX.X, op=ALU.max),
                              reads=[B_PN[z]], pwrites=[B_NM])

                    def prep(ti):
                        t0 = ti * 512
                        s = ti % 2
                        cx.dma("sp", xs[s][:], x_src[t0:t0 + 512, :].rearrange("(a p) d -> p a d", p=128),
                               reads=[cx.D(x_src_name, ti)], writes=[B_xs[s]])
                        cx.begin_fill(B_xT[s])
                        for kc in range(KC):
                            pi = kc % 2

                            def tr(e, kc=kc, pi=pi, s=s):
                                last = None
                                for a in range(4):
                                    last = e.transpose(out=ptr[pi][:, a * 128:(a + 1) * 128],
                                                       in_=xs[s][:, a, kc * 128:(kc + 1) * 128], identity=ident_f)
                                return last
                            cx.op("pe", tr, reads=[B_xs[s]] + CONSTS, writes=[B_ptr[pi]])
                            copy_op(evac_eng(), xT[s][:, kc, :], ptr[pi][:, :], reads=[B_ptr[pi]], pwrites=[B_xT[s]])

                    prep(0)
                    for ti in range(T // 512):
                        t0 = ti * 512
                        s = ti % 2
                        xTs = xT[s]
                        fm = []
                        for h in range(4):
                            fm.append((h * 128, 0.125, QT[h, :, t0:t0 + 512], "QT"))
                        for h in range(4):
                            fm.append((512 + h * 128, None, KT[h, :, t0:t0 + 512], "KT"))
                        for p in range(2):
                            fm.append((1536 + p * 128, None, LQT[p, :, t0:t0 + 512], "LQT"))
                        for p in range(2):
                            fm.append((1792 + p * 128, 0.125, LKT[p, :, t0:t0 + 512], "LKT"))
                        for fi, (c0, sc, dst, dn) in enumerate(fm):
                            pt, bpt = mm_group(lambda pt: pt[:, 0:512], lambda kc, c0=c0: Win[:, kc, c0:c0 + 128],
                                               lambda kc: xTs[:, kc, :], [B_Win, B_xT[s]])
                            i = ib[0] % 6
                            ib[0] += 1
                            copy_op(evac_eng(), stb[i][:, :], pt[:, 0:512], reads=[bpt], writes=[B_stb[i]], scale=sc)
                            cx.dma("pool", dst, stb[i][:, :], reads=[B_stb[i]], pwrites=[cx.D(dn, ti)])
                            if fi == 7 and ti + 1 < T // 512:
                                prep(ti + 1)
                            if wgen is not None and fi % 2 == 1:
                                try:
                                    next(wgen)
                                except StopIteration:
                                    wgen = None
                            if fi < 8:
                                pend.append((fi, i))
                            if len(pend) > 2:
                                norm_item(*pend.pop(0), ti)
                        while pend:
                            norm_item(*pend.pop(0), ti)
                        for g, dst in ((0, IGd), (1, FGd)):
                            pt, bpt = mm_group(lambda pt: pt[0:36, 0:512], lambda kc, g=g: WG[:, kc, g, :],
                                               lambda kc: xTs[:, kc, :], [B_WG, B_xT[s]])
                            j = jf[0] % 4
                            jf[0] += 1
                            cx.op("dve", lambda e, j=j, pt=pt, g=g: e.tensor_scalar(
                                out=stf[j][0:36, :], in0=pt[0:36, 0:512], scalar1=GBC[0:36, g:g + 1], scalar2=None, op0=ALU.add),
                                reads=[bpt, B_GBC], writes=[B_stf[j]])
                            cx.dma("pool", dst[0:4, t0:t0 + 512], stf[j][0:4, :], reads=[B_stf[j]], pwrites=[cx.D("G", ti)])
                            cx.dma("pool", dst[4:8, t0:t0 + 512], stf[j][32:36, :], reads=[B_stf[j]], pwrites=[cx.D("G", ti)])
                        for a in range(4):
                            r0 = t0 + a * 128
                            for (c0, wd, dst, dn, isf, sc) in ((1024, 512, VA, "VA", False, None), (2048, 512, LV, "LV", False, None),
                                                             (2560, 512, LO, "LO", True, None), (1792, 256, LK, "LK", False, 0.125)):
                                pt, bpt = mm_group(lambda pt, wd=wd: pt[:, 0:wd], lambda kc, a=a: xTs[:, kc, a * 128:(a + 1) * 128],
                                                   lambda kc, c0=c0, wd=wd: Win[:, kc, c0:c0 + wd], [B_Win, B_xT[s]])
                                if isf:
                                    j = jf[0] % 4
                                    jf[0] += 1
                                    copy_op(evac_eng(), stf[j][:, 0:wd], pt[:, 0:wd], reads=[bpt], writes=[B_stf[j]])
                                    cx.dma("pool", dst[r0:r0 + 128, :], stf[j][:, 0:wd], reads=[B_stf[j]], pwrites=[cx.D(dn, ti)])
                                else:
                                    i = ib[0] % 6
                                    ib[0] += 1
                                    copy_op(evac_eng(), stb[i][:, 0:wd], pt[:, 0:wd], reads=[bpt], writes=[B_stb[i]], scale=sc)
                                    cx.dma("pool", dst[r0:r0 + 128, :], stb[i][:, 0:wd], reads=[B_stb[i]], pwrites=[cx.D(dn, ti)])
                    if wgen is not None:
                        for _ in wgen:
                            pass
                    cx.barrier()

                def dtiles(name, s_off, S):
                    return [cx.D(name, ti) for ti in range(s_off // 512, (s_off + S) // 512)]

                with ExitStack() as ph:
                    ph.enter_context(nc.named_scope('A%d' % l))
                    Qs = [SB(ph, [128, SMAX], BF16) for _ in range(2)]
                    Ks = [SB(ph, [128, SMAX], BF16) for _ in range(2)]
                    Vs = [SB(ph, [128, NCH_MAX, 129], BF16) for _ in range(2)]
                    Ptp = [SB(ph, [128, 1024], BF16) for _ in range(3)]
                    Pt = [[Ptp[i][:, mp * 512:(mp + 1) * 512] for i in range(3)] for mp in range(2)]
                    TMPd = [SB(ph, [128, 3, 8, 129], F32) for _ in range(2)]
                    OAq = [SB(ph, [128, 4, 128], F32) for _ in range(2)]
                    OAs = SB(ph, [128, 4, 128], F32)
                    EPq = [SB(ph, [128, 24], F32) for _ in range(2)]
                    SMQ = SB(ph, [128, 56], F32)
                    BTd = [SB(ph, [128, 2, 2, 32], F32) for _ in range(2)]
                    EP = [SB(ph, [128, 16], F32) for _ in range(2)]
                    OA = [SB(ph, [128, 128], F32) for _ in range(2)]
                    ATS = [SB(ph, [128, 4, 128], BF16) for _ in range(2)]
                    OAsq = [SB(ph, [128, 128], F32) for _ in range(2)]
                    SCp = [PS(ph, [128, 1024]) for _ in range(2)]
                    SC = [[SCp[par][:, mp * 512:(mp + 1) * 512] for par in range(2)] for mp in range(2)]
                    ACC = [PS(ph, [128, 512]) for _ in range(3)]
                    PSM = PS(ph, [128, 512])
                    B_PSM = Buf()
                    B_Q, B_K, B_V = [Buf(), Buf()], [Buf(), Buf()], [Buf(), Buf()]
                    B_Pt = [[Buf() for _ in range(3)] for _ in range(2)]
                    B_MX, B_SM = Buf(), Buf()
                    B_TMPd, B_OAq, B_EPq = [Buf(), Buf()], [Buf(), Buf()], [Buf(), Buf()]
                    B_OAs = Buf()
                    pending_ep = [None]
                    qbc = [0]
                    B_BTd = [Buf(), Buf()]
                    B_EP = [Buf(), Buf()]
                    B_OA = [Buf(), Buf()]
                    B_ATS = [Buf(), Buf()]
                    B_OAsq = [Buf(), Buf()]
                    B_SC = [[Buf(), Buf()], [Buf(), Buf()]]
                    B_ACC = [Buf(), Buf(), Buf()]
                    for s in range(2):
                        cx.op("pool", lambda e, s=s: e.memset(Vs[s][:, :, 128:129], 1.0), pwrites=[B_V[s]])
                    hi = 0
                    bbt = lambda h: CF[:, coff["bb"][0] + h * 32: coff["bb"][0] + (h + 1) * 32]
                    bat = lambda h: CF[:, coff["ba"][0] + h * 32: coff["ba"][0] + (h + 1) * 32]
                    fbc = lambda h, k: CF[:, coff["fb"][0] + h * 4 + k: coff["fb"][0] + h * 4 + k + 1]
                    fac = lambda h, k: CF[:, coff["fa"][0] + h * 4 + k: coff["fa"][0] + h * 4 + k + 1]
                    for (s_off, S) in seqs:
                        nqb, nst = S // 512, S // 128
                        ti0, ti1 = s_off // 512, (s_off + S) // 512
                        cx.op("dve", lambda e: e.tensor_reduce(out=SMQ[:, 0:8], in_=NM[:, :, ti0:ti1], axis=AX.X, op=ALU.max),
                              reads=[B_NM], writes=[B_SM])
                        cx.op("dve", lambda e: e.tensor_tensor(out=SMQ[:, 8:12], in0=SMQ[:, 0:4], in1=SMQ[:, 4:8], op=ALU.mult),
                              reads=[B_SM], pwrites=[B_SM])

                        def hb(e):
                            e.matmul(PSM[:, 0:4], lhsT=HSEL[:, 0, :], rhs=SMQ[:, 8:12], start=True, stop=True, skip_group_check=True)
                            return e.matmul(PSM[:, 4:8], lhsT=HSEL[:, 1, :], rhs=SMQ[:, 8:12], start=True, stop=True,
                                            skip_group_check=True)
                        cx.op("pe", hb, reads=[B_SM, B_BLK], writes=[B_PSM])
                        cx.op("dve", lambda e: e.tensor_copy(out=SMQ[:, 16:24], in_=PSM[:, 0:8]), reads=[B_PSM, B_SM], pwrites=[B_SM])
                        cx.op("act", lambda e: e.activation(out=SMQ[:, 24:32], in_=SMQ[:, 16:24], func=AF.Ln), reads=[B_SM],
                              pwrites=[B_SM])
                        cx.op("act", lambda e: e.activation(out=SMQ[:, 32:40], in_=SMQ[:, 24:32], func=AF.Exp, scale=0.5),
                              reads=[B_SM], pwrites=[B_SM])
                        cx.op("dve", lambda e: e.tensor_scalar(out=SMQ[:, 40:48], in0=SMQ[:, 32:40], scalar1=-1.0, scalar2=None,
                                                               op0=ALU.mult), reads=[B_SM], pwrites=[B_SM])
                        cx.op("dve", lambda e: e.tensor_tensor(out=SMQ[:, 48:52], in0=SMQ[:, 40:44], in1=SMQ[:, 44:48], op=ALU.min),
                              reads=[B_SM], pwrites=[B_SM])
                        for h in range(4):
                            s = hi % 2
                            hi += 1
                            Q, K, V = Qs[s], Ks[s], Vs[s]
                            BT, B_BT = BTd[s], B_BTd[s]
                            cx.dma("sp", Q[:, 0:S], QT[h, :, s_off:s_off + S], reads=dtiles("QT", s_off, S), writes=[B_Q[s]])
                            cx.dma("sp", K[:, 0:S], KT[h, :, s_off:s_off + S], reads=dtiles("KT", s_off, S), writes=[B_K[s]])
                            cx.begin_fill(B_V[s])
                            for c0 in range(0, nst, 8):
                                c1 = min(c0 + 8, nst)
                                cx.dma("sp", V[:, c0:c1, 0:128],
                                       VA[s_off + c0 * 128:s_off + c1 * 128, h * 128:(h + 1) * 128].rearrange(
                                           "(c p) d -> p c d", p=128),
                                       reads=dtiles("VA", s_off, S), pwrites=[B_V[s]])
                            cx.begin_fill(B_BT)
                            for mp in range(2):
                                cx.op("dve", lambda e, mp=mp: e.tensor_scalar(out=BT[:, 0, mp, :], in0=bbt(h),
                                                                            scalar1=SMQ[:, 48 + h:49 + h], scalar2=None, op0=ALU.add),
                                      reads=[B_SM] + CONSTS, pwrites=[B_BT])
                                cx.op("dve", lambda e, mp=mp: e.tensor_scalar(out=BT[:, 1, mp, :], in0=bat(h),
                                                                            scalar1=SMQ[:, 48 + h:49 + h], scalar2=None, op0=ALU.add),
                                      reads=[B_SM] + CONSTS, pwrites=[B_BT])
                            for qb in range(nqb):
                                units = []
                                for ts in range(0, 4 * qb):
                                    units.append(("b", ts))
                                for ts in range(4 * qb, 4 * qb + 4):
                                    units.append(("i", ts))
                                for ts in range(4 * qb + 4, nst):
                                    units.append(("a", ts))
                                nu = len(units)
                                qsl = slice(qb * 512, (qb + 1) * 512)

                                def emit_qk(i):
                                    kind, ts = units[i]
                                    par = i % 2
                                    ksl = slice(ts * 128, (ts + 1) * 128)

                                    def f(e):
                                        fin = (kind != "i")
                                        e.matmul(SC[0][par][:, :], lhsT=K[0:64, ksl], rhs=Q[0:64, qsl], start=True, stop=fin)
                                        last = e.matmul(SC[1][par][:, :], lhsT=K[64:128, ksl], rhs=Q[64:128, qsl], start=True,
                                                        stop=fin)
                                        if kind == "i":
                                            d = ts - 4 * qb
                                            for mp in range(2):
                                                last = e.matmul(SC[mp][par][:, :], lhsT=NI16(h), rhs=D16(d), start=False, stop=True,
                                                                skip_group_check=True)
                                        return last
                                    cx.op("pe", f, reads=[B_Q[s], B_K[s]] + CONSTS, writes=[B_SC[0][par], B_SC[1][par]])

                                def emit_exp(i):
                                    kind, ts = units[i]
                                    par = i % 2
                                    pi = i % 3
                                    if kind == "b":
                                        bias = BT[:, 0, 0, 4 * qb - ts: 4 * qb - ts + 1]
                                    elif kind == "a":
                                        bias = BT[:, 1, 0, ts - 4 * qb: ts - 4 * qb + 1]
                                    else:
                                        bias = SMQ[:, 48 + h:49 + h]
                                    cx.op("act", lambda e: e.activation(out=Ptp[pi][:, :], in_=SCp[par][:, :], func=AF.Exp, bias=bias,
                                                                        scale=1.0),
                                          reads=[B_SC[0][par], B_SC[1][par], B_BT, B_SM], writes=[B_Pt[0][pi], B_Pt[1][pi]])

                                def emit_av(i, first, lastu):
                                    kind, ts = units[i]
                                    pi = i % 3

                                    def f(e):
                                        last = None
                                        for k in range(4):
                                            for mp in range(2):
                                                idx = k * 2 + mp
                                                bank, col = idx // 3, (idx % 3) * 129
                                                last = e.matmul(ACC[bank][:, col:col + 129], lhsT=Pt[mp][pi][:, k * 128:(k + 1) * 128],
                                                                rhs=V[:, ts, :], start=(first and idx % 3 == 0), stop=lastu,
                                                                skip_group_check=True)
                                        return last
                                    cx.op("pe", f, reads=[B_Pt[0][pi], B_Pt[1][pi], B_V[s]], writes=B_ACC)

                                def emit_phase_end(kind):
                                    phi = "bia".index(kind)
                                    for bank in range(3):
                                        n = 3 if bank < 2 else 2
                                        cx.op("dve", lambda e, bank=bank, n=n, phi=phi: e.tensor_copy(
                                            out=TMP[:, phi, 3 * bank:3 * bank + n, :].rearrange("p a c -> p (a c)"),
                                            in_=ACC[bank][:, 0:n * 129]), reads=[B_ACC[bank]], pwrites=[B_TMP])

                                zq = qbc[0] % 2
                                qbc[0] += 1
                                TMP, B_TMP = TMPd[zq], B_TMPd[zq]
                                cx.begin_fill(B_TMP)
                                has_b = qb > 0
                                has_a = 4 * qb + 4 < nst
                                emit_qk(0)
                                deferred_av = []
                                for i in range(nu):
                                    kind = units[i][0]
                                    emit_exp(i)
                                    if i + 1 < nu:
                                        emit_qk(i + 1)
                                    for fn in deferred_av:
                                        fn()
                                    deferred_av = []
                                    first = (i == 0) or (units[i - 1][0] != kind)
                                    lastu = (i == nu - 1) or (units[i + 1][0] != kind)

                                    def do_av(i=i, first=first, lastu=lastu, kind=kind):
                                        emit_av(i, first, lastu)
                                        if lastu:
                                            emit_phase_end(kind)
                                    if first and i > 0 and not lastu:
                                        deferred_av.append(do_av)
                                    else:
                                        do_av()
                                    if pending_ep[0] is not None:
                                        try:
                                            next(pending_ep[0])
                                        except StopIteration:
                                            pending_ep[0] = None
                                for fn in deferred_av:
                                    fn()

                                def epilogue(h=h, zq=zq, TMP=TMP, B_TMP=B_TMP, has_b=has_b, has_a=has_a, r0=s_off + qb * 512):
                                    oaq, boaq, epq, bepq = OAq[zq], B_OAq[zq], EPq[zq], B_EPq[zq]
                                    n_ops = 0
                                    for k in range(4):
                                        for mp in range(2):
                                            idx = k * 2 + mp
                                            if has_b:
                                                cx.op("dve", lambda e, idx=idx, k=k: e.scalar_tensor_tensor(
                                                    out=TMP[:, 1, idx, :], in0=TMP[:, 0, idx, :], scalar=fbc(h, k), in1=TMP[:, 1, idx, :],
                                                    op0=ALU.mult, op1=ALU.add), reads=[B_TMP] + CONSTS, pwrites=[B_TMP])
                                            if has_a:
                                                cx.op("dve", lambda e, idx=idx, k=k: e.scalar_tensor_tensor(
                                                    out=TMP[:, 1, idx, :], in0=TMP[:, 2, idx, :], scalar=fac(h, k), in1=TMP[:, 1, idx, :],
                                                    op0=ALU.mult, op1=ALU.add), reads=[B_TMP] + CONSTS, pwrites=[B_TMP])
                                        yield
                                    cx.op("dve", lambda e: e.reciprocal(out=epq[:, 0:8], in_=TMP[:, 1, 0:8, 128]),
                                          reads=[B_TMP], writes=[bepq])
                                    cx.op("dve", lambda e: e.tensor_scalar(out=epq[:, 8:12], in0=epq[:, 1:8:2], scalar1=NEGLAM,
                                                                           scalar2=None, op0=ALU.mult),
                                          reads=[bepq, B_LAMC], pwrites=[bepq])
                                    yield
                                    cx.begin_fill(boaq)
                                    for k in range(4):
                                        cx.op("dve", lambda e, k=k: e.tensor_scalar(
                                            out=oaq[:, k, :], in0=TMP[:, 1, 2 * k, 0:128], scalar1=epq[:, 2 * k:2 * k + 1], scalar2=None,
                                            op0=ALU.mult), reads=[bepq, B_TMP], pwrites=[boaq])
                                        cx.op("dve", lambda e, k=k: e.scalar_tensor_tensor(
                                            out=oaq[:, k, :], in0=TMP[:, 1, 2 * k + 1, 0:128], scalar=epq[:, 8 + k:9 + k], in1=oaq[:, k, :],
                                            op0=ALU.mult, op1=ALU.add), reads=[bepq, B_TMP, boaq], pwrites=[boaq])
                                        yield
                                    cx.op("dve", lambda e: e.tensor_tensor(out=OAs[:, :, :], in0=oaq[:, :, :], in1=oaq[:, :, :],
                                                                           op=ALU.mult), reads=[boaq], writes=[B_OAs])
                                    cx.op("dve", lambda e: e.tensor_reduce(out=epq[:, 12:16], in_=OAs[:, :, :], axis=AX.X, op=ALU.add),
                                          reads=[B_OAs, bepq], pwrites=[bepq])
                                    cx.op("dve", lambda e: e.tensor_scalar(out=epq[:, 16:20], in0=epq[:, 12:16], scalar1=1.0 / 128.0,
                                                                           scalar2=EPS, op0=ALU.mult, op1=ALU.add),
                                          reads=[bepq], pwrites=[bepq])
                                    yield
                                    yield
                                    cx.op("act", lambda e: e.activation(out=epq[:, 16:20], in_=epq[:, 16:20], func=AF.Ln), reads=[bepq],
                                          pwrites=[bepq])
                                    cx.op("act", lambda e: e.activation(out=epq[:, 20:24], in_=epq[:, 16:20], func=AF.Exp, scale=-0.5),
                                          reads=[bepq], pwrites=[bepq])
                                    yield
                                    cx.begin_fill(B_ATS[zq])
                                    for k in range(4):
                                        cx.op("dve", lambda e, k=k: e.scalar_tensor_tensor(
                                            out=ATS[zq][:, k, :], in0=oaq[:, k, :], scalar=epq[:, 20 + k:21 + k],
                                            in1=GN[:, 0, h * 128:(h + 1) * 128], op0=ALU.mult, op1=ALU.mult),
                                            reads=[bepq, boaq, B_GN], pwrites=[B_ATS[zq]])
                                    cx.dma("pool", MIX[r0:r0 + 512, h * 128:(h + 1) * 128].rearrange("(k p) d -> p k d", p=128),
                                           ATS[zq][:, :, :], reads=[B_ATS[zq]], pwrites=[cx.D("MIX", r0 // 512)])

                                if pending_ep[0] is not None:
                                    for _ in pending_ep[0]:
                                        pass
                                pending_ep[0] = epilogue()
                    if pending_ep[0] is not None:
                        for _ in pending_ep[0]:
                            pass
                        pending_ep[0] = None
                    cx.barrier()

                with ExitStack() as ph:
                    ph.enter_context(nc.named_scope('L%d' % l))
                    IG = SB(ph, [64, SMAX], F32)
                    FG = SB(ph, [64, SMAX], F32)
                    TA = SB(ph, [64, SMAX], F32)
                    RM = SB(ph, [64, SMAX], BF16)
                    SML = SB(ph, [64, 8, NCH_MAX], F32)
                    ETM = SB(ph, [128, NCH_MAX, 36], F32)
                    FTM = SB(ph, [128, NCH_MAX, 36], F32)
                    BBC = SB(ph, [128, 8, NCH_MAX], F32)
                    LQs = SB(ph, [128, SMAX], BF16)
                    LKs = SB(ph, [128, SMAX], BF16)
                    LKh = SB(ph, [128, NCH_MAX, 64], BF16)
                    LVh = SB(ph, [128, NCH_MAX, 129], BF16)
                    LOh = SB(ph, [128, NCH_MAX, 128], F32)
                    HD = [SB(ph, [128, NCH_MAX, 129], F32) for _ in range(2)]
                    RD = SB(ph, [128, 2, 4, NCH_MAX], F32)
                    B_HD = [Buf(), Buf()]
                    B_RD = Buf()
                    MXL = SB(ph, [128, NCH_MAX, 128], BF16)
                    CST = SB(ph, [128, 2, 129], F32)
                    CSB = [SB(ph, [128, 2, 129], BF16) for _ in range(2)]
                    SMT = [SB(ph, [128, 128], BF16) for _ in range(4)]
                    KKT = [SB(ph, [128, 128], BF16) for _ in range(4)]
                    DN = [SB(ph, [128, 4], F32) for _ in range(4)]
                    JK = SB(ph, [128, 128], F32)
                    SS = SB(ph, [128, 3, NCH_MAX], F32)
                    T1 = [SB(ph, [128, 128], F32) for _ in range(2)]
                    PSTb = [PS(ph, [128, 512]) for _ in range(2)]
                    POb = [PS(ph, [128, 512]) for _ in range(2)]
                    PUb = [PS(ph, [128, 512]) for _ in range(2)]
                    CSBq = [SB(ph, [128, 129], BF16) for _ in range(4)]
                    B_PSTq = [Buf() for _ in range(2)]
                    B_POq = [Buf() for _ in range(2)]
                    B_PUq = [Buf() for _ in range(2)]
                    B_CSBq = [Buf() for _ in range(4)]
                    B_CSTd = [Buf(), Buf()]
                    PTR = PS(ph, [128, 512])
                    PBB = PS(ph, [128, 512])
                    B_IG, B_FG, B_TA, B_RM, B_SML, B_ETM, B_FTM, B_BBC = (Buf() for _ in range(8))
                    B_LQ, B_LK, B_LKh, B_LVh, B_LOh, B_H_unused, B_MXL, B_CST, B_JK, B_SS = (Buf() for _ in range(10))
                    B_CSB = [Buf(), Buf()]
                    B_SMT = [Buf() for _ in range(4)]
                    B_KKT = [Buf() for _ in range(4)]
                    B_DN = [Buf() for _ in range(4)]
                    B_T1 = [Buf(), Buf()]
                    B_PTR, B_PBB = Buf(), Buf()
                    cx.op("pool", lambda e: e.memset(IG[:], 0.0), writes=[B_IG])
                    cx.op("pool", lambda e: e.memset(FG[:], 0.0), writes=[B_FG])
                    cx.op("pool", lambda e: e.memset(TA[:], 0.0), writes=[B_TA])
                    cx.op("pool", lambda e: e.memset(SML[:], 0.0), writes=[B_SML])
                    cx.op("dve", lambda e: e.memset(RM[:], 1.0), writes=[B_RM])
                    cx.op("dve", lambda e: e.memset(RM[0:4, 0:SMAX:128], 0.0), reads=[B_RM], pwrites=[B_RM])
                    cx.op("dve", lambda e: e.memset(RM[32:36, 127:SMAX:128], 0.0), reads=[B_RM], pwrites=[B_RM])
                    cx.op("pool", lambda e: e.memset(LVh[:, :, 128:129], 1.0), pwrites=[B_LVh])
                    for i_ in range(4):
                        cx.op("pool", lambda e, i_=i_: e.memset(KKT[i_][:, :], 0.0), writes=[B_KKT[i_]])
                    mask_ap = (cf("mf"), cf("mb"))
                    selc = lambda r: CF[0:36, coff["sel"][0] + r * 64: coff["sel"][0] + (r + 1) * 64]
                    stp = [0]
                    for (s_off, S) in seqs:
                        nch = S // 128
                        cx.begin_fill(B_IG)
                        cx.begin_fill(B_FG)
                        cx.dma("sp", IG[0:4, 0:S], IGd[0:4, s_off:s_off + S], reads=dtiles("G", s_off, S), pwrites=[B_IG])
                        cx.dma("sp", IG[32:36, 0:S], IGd[4:8, s_off:s_off + S], reads=dtiles("G", s_off, S), pwrites=[B_IG])
                        cx.dma("sp", FG[0:4, 0:S], FGd[0:4, s_off:s_off + S], reads=dtiles("G", s_off, S), pwrites=[B_FG])
                        cx.dma("sp", FG[32:36, 0:S], FGd[4:8, s_off:s_off + S], reads=dtiles("G", s_off, S), pwrites=[B_FG])
                        A_, F_, I_ = TA[0:36, 0:S], FG[0:36, 0:S], IG[0:36, 0:S]
                        v3 = lambda ap: ap.rearrange("p (c j) -> p c j", j=128)
                        cx.op("dve", lambda e: e.scalar_tensor_tensor(out=A_, in0=F_, scalar=-1.0, in1=F_, op0=ALU.mult, op1=ALU.min),
                              reads=[B_FG], writes=[B_TA])
                        cx.op("act", lambda e: e.activation(out=A_, in_=A_, func=AF.Exp), reads=[B_TA], writes=[B_TA])
                        cx.op("act", lambda e: e.activation(out=A_, in_=A_, func=AF.Ln, bias=1.0), reads=[B_TA], writes=[B_TA])
                        cx.op("dve", lambda e: e.scalar_tensor_tensor(out=F_, in0=F_, scalar=0.0, in1=A_, op0=ALU.min, op1=ALU.subtract),
                              reads=[B_FG, B_TA], writes=[B_FG])
                        cx.op("dve", lambda e: e.tensor_tensor_scan(out=TA[0:4, 0:S], data0=RM[0:4, 0:S], data1=FG[0:4, 0:S],
                                                                    initial=0.0, op0=ALU.mult, op1=ALU.add),
                              reads=[B_FG, B_RM], writes=[B_TA])
                        cx.op("dve", lambda e: e.tensor_tensor_scan(out=TA[32:36, S - 1::-1] if False else TA[32:36, 0:S][:, ::-1],
                                                                    data0=RM[32:36, 0:S][:, ::-1], data1=FG[32:36, 0:S][:, ::-1],
                                                                    initial=0.0, op0=ALU.mult, op1=ALU.add),
                              reads=[B_FG, B_RM, B_TA], pwrites=[B_TA])
                        cx.op("dve", lambda e: e.tensor_tensor(out=I_, in0=I_, in1=A_, op=ALU.subtract), reads=[B_IG, B_TA], writes=[B_IG])
                        WM, GC, MS, MPV, MC, BETA = (SML[0:36, i, 0:nch] for i in range(6))
                        cx.op("dve", lambda e: e.tensor_reduce(out=WM, in_=v3(I_), axis=AX.X, op=ALU.max), reads=[B_IG], writes=[B_SML])
                        cx.op("dve", lambda e: e.tensor_copy(out=SML[0:4, 1, 0:nch], in_=TA[0:4, 127:S:128]), reads=[B_TA, B_SML],
                              pwrites=[B_SML])
                        cx.op("dve", lambda e: e.tensor_copy(out=SML[32:36, 1, 0:nch], in_=TA[32:36, 0:S:128]), reads=[B_TA, B_SML],
                              pwrites=[B_SML])
                        cx.op("dve", lambda e: e.tensor_tensor_scan(out=SML[0:4, 2, 0:nch], data0=SML[0:4, 0, 0:nch],
                                                                    data1=SML[0:4, 1, 0:nch], initial=0.0, op0=ALU.max, op1=ALU.add),
                              reads=[B_SML], pwrites=[B_SML])
                        cx.op("dve", lambda e: e.tensor_tensor_scan(out=SML[32:36, 2, 0:nch][:, ::-1], data0=SML[32:36, 0, 0:nch][:, ::-1],
                                                                    data1=SML[32:36, 1, 0:nch][:, ::-1], initial=0.0, op0=ALU.max,
                                                                    op1=ALU.add), reads=[B_SML], pwrites=[B_SML])
                        cx.op("dve", lambda e: e.memset(SML[0:36, 3, 0:nch], 0.0), reads=[B_SML], pwrites=[B_SML])
                        if nch > 1:
                            cx.op("dve", lambda e: e.tensor_copy(out=SML[0:4, 3, 1:nch], in_=SML[0:4, 2, 0:nch - 1]), reads=[B_SML],
                                  pwrites=[B_SML])
                            cx.op("dve", lambda e: e.tensor_copy(out=SML[32:36, 3, 0:nch - 1], in_=SML[32:36, 2, 1:nch]), reads=[B_SML],
                                  pwrites=[B_SML])
                        cx.op("dve", lambda e: e.tensor_tensor(out=MC, in0=MPV, in1=WM, op=ALU.max), reads=[B_SML], pwrites=[B_SML])
                        cx.op("dve", lambda e: e.tensor_tensor(out=BETA, in0=MPV, in1=MC, op=ALU.subtract), reads=[B_SML], pwrites=[B_SML])
                        cx.op("act", lambda e: e.activation(out=BETA, in_=BETA, func=AF.Exp), reads=[B_SML], pwrites=[B_SML])
                        mcb = MC.unsqueeze(2).to_broadcast([36, nch, 128])
                        cx.op("dve", lambda e: e.tensor_tensor(out=v3(I_), in0=v3(I_), in1=mcb, op=ALU.subtract), reads=[B_IG, B_SML],
                              writes=[B_IG])
                        cx.op("act", lambda e: e.activation(out=I_, in_=I_, func=AF.Exp), reads=[B_IG], writes=[B_IG])
                        cx.op("dve", lambda e: e.scalar_tensor_tensor(out=v3(A_), in0=v3(A_), scalar=-1.0, in1=mcb, op0=ALU.mult,
                                                                      op1=ALU.subtract), reads=[B_TA, B_SML], writes=[B_TA])
                        cx.op("act", lambda e: e.activation(out=A_, in_=A_, func=AF.Exp), reads=[B_TA], writes=[B_TA])
                        for (src, bsrc, dst, bdst) in ((IG, B_IG, ETM, B_ETM), (TA, B_TA, FTM, B_FTM)):
                            cx.begin_fill(bdst)
                            for c0 in range(0, nch, 8):
                                n8 = min(8, nch - c0)

                                def trf(e, src=src, c0=c0, n8=n8):
                                    last = None
                                    for c in range(n8):
                                        last = e.transpose(out=PTR[:, c * 36:(c + 1) * 36], in_=src[0:36, (c0 + c) * 128:(c0 + c + 1) * 128],
                                                           identity=ident_f[0:36, 0:36])
                                    return last
                                cx.op("pe", trf, reads=[bsrc] + CONSTS, writes=[B_PTR])
                                cx.op("dve", lambda e, dst=dst, c0=c0, n8=n8: e.tensor_copy(
                                    out=dst[:, c0:c0 + n8, :].rearrange("p c r -> p (c r)"), in_=PTR[:, 0:n8 * 36]),
                                    reads=[B_PTR], pwrites=[bdst])

                        def bbf(e):
                            last = None
                            for r in range(8):
                                hp = (r % 4) % 2
                                last = e.matmul(PBB[hp * 64:(hp + 1) * 64, r * nch:(r + 1) * nch], lhsT=selc(r), rhs=BETA, start=True,
                                                stop=True, skip_group_check=True)
                            return last
                        cx.op("pe", bbf, reads=[B_SML] + CONSTS, writes=[B_PBB])
                        cx.begin_fill(B_BBC)
                        for hp in range(2):
                            for r in range(8):
                                if (r % 4) % 2 != hp:
                                    continue
                                cx.op("dve", lambda e, hp=hp, r=r: e.tensor_copy(out=BBC[hp * 64:(hp + 1) * 64, r, 0:nch],
                                                                               in_=PBB[hp * 64:(hp + 1) * 64, r * nch:(r + 1) * nch]),
                                      reads=[B_PBB], pwrites=[B_BBC])
                        for h in range(4):
                            hp = h % 2
                            P0 = hp * 64
                            if hp == 0:
                                cx.dma("sp", LQs[:, 0:S], LQT[h // 2, :, s_off:s_off + S], reads=dtiles("LQT", s_off, S), writes=[B_LQ])
                                cx.dma("sp", LKs[:, 0:S], LKT[h // 2, :, s_off:s_off + S], reads=dtiles("LKT", s_off, S), writes=[B_LK])
                            cx.begin_fill(B_LKh)
                            cx.begin_fill(B_LVh)
                            cx.begin_fill(B_LOh)
                            for c0 in range(0, nch, 8):
                                c1 = min(c0 + 8, nch)
                                rs = slice(s_off + c0 * 128, s_off + c1 * 128)
                                cx.dma("sp", LKh[:, c0:c1, :], LK[rs, h * 64:(h + 1) * 64].rearrange("(c p) d -> p c d", p=128),
                                       reads=dtiles("LK", s_off, S), pwrites=[B_LKh])
                                cx.dma("sp", LVh[:, c0:c1, 0:128], LV[rs, h * 128:(h + 1) * 128].rearrange("(c p) d -> p c d", p=128),
                                       reads=dtiles("LV", s_off, S), pwrites=[B_LVh])
                                cx.dma("sp", LOh[:, c0:c1, :], LO[rs, h * 128:(h + 1) * 128].rearrange("(c p) d -> p c d", p=128),
                                       reads=dtiles("LO", s_off, S), pwrites=[B_LOh])
                            cx.op("dve", lambda e: e.memset(CST[:], 0.0), writes=[B_CSTd[0], B_CSTd[1]])
                            for i_ in range(4):
                                cx.op("pool", lambda e, i_=i_: e.memset(CSBq[i_][(1 - hp) * 64:(2 - hp) * 64, :], 0.0),
                                      writes=[B_CSBq[i_]])
                            cx.begin_fill(B_HD[0])
                            cx.begin_fill(B_HD[1])
                            NS = 2 * nch

                            def step_info(k):
                                i, d = k // 2, k % 2
                                c = i if d == 0 else nch - 1 - i
                                r = d * 4 + h
                                rc = r if r < 4 else 32 + (r - 4)
                                return c, d, r, rc

                            def stage_a(k):
                                c, d, r, rc = step_info(k)
                                s4, s2 = k % 4, k % 2
                                csl = slice(c * 128, (c + 1) * 128)
                                ecol = ETM[:, c, rc:rc + 1]
                                cx.op("pe", lambda e: e.matmul(PSTb[s2][:, 0:128], lhsT=LKs[P0:P0 + 64, csl],
                                                               rhs=LQs[P0:P0 + 64, csl], start=True, stop=True, skip_group_check=True),
                                      reads=[B_LK, B_LQ], writes=[B_PSTq[s2]])
                                cx.op("act", lambda e: e.activation(out=KKT[s4][:, P0:P0 + 64], in_=LKh[:, c, :], func=AF.Copy, scale=ecol),
                                      reads=[B_LKh, B_ETM], writes=[B_KKT[s4]])
                                cx.op("pe", lambda e: e.matmul(PUb[s2][:, 0:129], lhsT=KKT[s4][:, :],
                                                               rhs=LVh[:, c, :], start=True, stop=True, skip_group_check=True),
                                      reads=[B_KKT[s4], B_LVh], writes=[B_PUq[s2]])

                            def stage_b(k):
                                c, d, r, rc = step_info(k)
                                s4, s2 = k % 4, k % 2
                                csl = slice(c * 128, (c + 1) * 128)
                                bcol = BBC[P0:P0 + 64, r, c:c + 1]
                                ecol = ETM[:, c, rc:rc + 1]
                                cx.op("act", lambda e: e.activation(out=CSBq[s4][P0:P0 + 64, :], in_=CST[P0:P0 + 64, d, :], func=AF.Copy,
                                                                    scale=bcol), reads=[B_CSTd[d], B_BBC], writes=[B_CSBq[s4]])
                                cx.op("dve", lambda e: e.scalar_tensor_tensor(
                                    out=SMT[s4][:, :], in0=PSTb[s2][:, 0:128], scalar=ecol, in1=mask_ap[d], op0=ALU.mult,
                                    op1=ALU.mult), reads=[B_PSTq[s2], B_ETM] + CONSTS, writes=[B_SMT[s4]])

                                def of(e):
                                    e.matmul(POb[s2][:, 0:129], lhsT=LQs[:, csl], rhs=CSBq[s4][:, :],
                                             start=True, stop=False, skip_group_check=True)
                                    return e.matmul(POb[s2][:, 0:129], lhsT=SMT[s4][:, :], rhs=LVh[:, c, :], start=False,
                                                    stop=True, skip_group_check=True)
                                cx.op("pe", of, reads=[B_LQ, B_CSBq[s4], B_SMT[s4], B_LVh], writes=[B_POq[s2]])
                                cx.op("dve", lambda e: e.scalar_tensor_tensor(
                                    out=CST[P0:P0 + 64, d, :], in0=CST[P0:P0 + 64, d, :], scalar=bcol,
                                    in1=PUb[s2][P0:P0 + 64, 0:129], op0=ALU.mult, op1=ALU.add),
                                    reads=[B_CSTd[d], B_BBC, B_PUq[s2]], writes=[B_CSTd[d]])

                            def stage_c(k):
                                c, d, r, rc = step_info(k)
                                s2 = k % 2
                                cx.op("dve", lambda e: e.tensor_copy(out=HD[d][:, c, :], in_=POb[s2][:, 0:129]), reads=[B_POq[s2]],
                                      pwrites=[B_HD[d]])

                            for k in range(NS + 2):
                                if k < NS:
                                    stage_a(k)
                                if 0 <= k - 1 < NS:
                                    stage_b(k - 1)
                                if 0 <= k - 2 < NS:
                                    stage_c(k - 2)
                            for d in range(2):
                                r = d * 4 + h
                                rc = r if r < 4 else 32 + (r - 4)
                                den = HD[d][:, 0:nch, 128]
                                cx.op("dve", lambda e, d=d, den=den: e.tensor_scalar(out=RD[:, d, 0, 0:nch], in0=den, scalar1=-1.0,
                                                                                    scalar2=None, op0=ALU.mult),
                                      reads=[B_HD[d]], writes=[B_RD] if d == 0 else (), pwrites=[B_RD] if d == 1 else ())
                                cx.op("dve", lambda e, d=d, den=den: e.tensor_tensor(out=RD[:, d, 1, 0:nch], in0=RD[:, d, 0, 0:nch], in1=den,
                                                                                    op=ALU.max), reads=[B_HD[d], B_RD], pwrites=[B_RD])
                                cx.op("dve", lambda e, d=d, rc=rc: e.tensor_tensor(out=RD[:, d, 2, 0:nch], in0=RD[:, d, 1, 0:nch],
                                                                                  in1=FTM[:, 0:nch, rc], op=ALU.max),
                                      reads=[B_FTM, B_RD], pwrites=[B_RD])
                                cx.op("dve", lambda e, d=d: e.reciprocal(out=RD[:, d, 3, 0:nch], in_=RD[:, d, 2, 0:nch]), reads=[B_RD],
                                      pwrites=[B_RD])
                            H = HD[0]
                            B_H = B_HD[0]
                            cx.begin_fill(B_SS)
                            for c in range(nch):
                                cx.op("dve", lambda e, c=c: e.tensor_scalar(out=HD[0][:, c, 0:128], in0=HD[0][:, c, 0:128],
                                                                            scalar1=RD[:, 0, 3, c:c + 1], scalar2=None, op0=ALU.mult),
                                      reads=[B_HD[0], B_RD], pwrites=[B_HD[0]])
                                cx.op("dve", lambda e, c=c: e.scalar_tensor_tensor(out=HD[0][:, c, 0:128], in0=HD[1][:, c, 0:128],
                                                                                   scalar=RD[:, 1, 3, c:c + 1], in1=HD[0][:, c, 0:128],
                                                                                   op0=ALU.mult, op1=ALU.add),
                                      reads=[B_HD[0], B_HD[1], B_RD], pwrites=[B_HD[0]])
                                cx.op("act", lambda e, c=c: e.activation(out=JK[:, :], in_=H[:, c, 0:128], func=AF.Square,
                                                                         accum_out=SS[:, 0, c:c + 1]), reads=[B_H], writes=[B_JK],
                                      pwrites=[B_SS])
                            cx.op("dve", lambda e: e.tensor_scalar(out=SS[:, 1, 0:nch], in0=SS[:, 0, 0:nch], scalar1=1.0 / 128.0,
                                                                   scalar2=EPS, op0=ALU.mult, op1=ALU.add), reads=[B_SS], pwrites=[B_SS])
                            cx.op("act", lambda e: e.activation(out=SS[:, 1, 0:nch], in_=SS[:, 1, 0:nch], func=AF.Ln), reads=[B_SS],
                                  pwrites=[B_SS])
                            cx.op("act", lambda e: e.activation(out=SS[:, 2, 0:nch], in_=SS[:, 1, 0:nch], func=AF.Exp, scale=-0.5),
                                  reads=[B_SS], pwrites=[B_SS])
                            cx.op("act", lambda e: e.activation(out=LOh[:, 0:nch, :], in_=LOh[:, 0:nch, :], func=AF.Sigmoid),
                                  reads=[B_LOh], writes=[B_LOh])
                            cx.begin_fill(B_MXL)
                            for c in range(nch):
                                y = c % 2
                                cx.op("dve", lambda e, c=c, y=y: e.scalar_tensor_tensor(
                                    out=T1[y][:, :], in0=H[:, c, 0:128], scalar=SS[:, 2, c:c + 1], in1=GN[:, 1, h * 128:(h + 1) * 128],
                                    op0=ALU.mult, op1=ALU.mult), reads=[B_H, B_SS, B_GN], writes=[B_T1[y]])
                                cx.op("pool", lambda e, c=c, y=y: e.tensor_tensor(out=MXL[:, c, :], in0=T1[y][:, :], in1=LOh[:, c, :],
                                                                                  op=ALU.mult), reads=[B_T1[y], B_LOh], pwrites=[B_MXL])
                            for c0 in range(0, nch, 4):
                                rs = slice(s_off + c0 * 128, s_off + (c0 + 4) * 128)
                                cx.dma("pool", MIX[rs, 512 + h * 128: 512 + (h + 1) * 128].rearrange("(c p) d -> p c d", p=128),
                                       MXL[:, c0:c0 + 4, :], reads=[B_MXL], pwrites=[cx.D("MIX", (s_off + c0 * 128) // 512)])
                    cx.barrier()
                if debug == "mix":
                    break

                def ln_part1(z, bz, st6, mv, bsm):
                    cx.op("dve", lambda e: e.bn_stats(out=st6[:, 0:6], in_=z[:, 0:512]), reads=[bz], writes=[bsm])
                    cx.op("dve", lambda e: e.bn_stats(out=st6[:, 6:12], in_=z[:, 512:1024]), reads=[bz, bsm], pwrites=[bsm])
                    cx.op("dve", lambda e: e.bn_aggr(out=mv[:, 0:2], in_=st6[:, 0:12]), reads=[bsm], pwrites=[bsm])
                    cx.op("dve", lambda e: e.tensor_scalar(out=mv[:, 2:3], in0=mv[:, 1:2], scalar1=EPS, scalar2=None, op0=ALU.add),
                          reads=[bsm], pwrites=[bsm])

                def ln_part2(z, bz, y, by, gi, mv, bsm):
                    cx.op("act", lambda e: e.activation(out=mv[:, 3:4], in_=mv[:, 2:3], func=AF.Ln), reads=[bsm], pwrites=[bsm])
                    cx.op("act", lambda e: e.activation(out=mv[:, 4:5], in_=mv[:, 3:4], func=AF.Exp, scale=-0.5), reads=[bsm], pwrites=[bsm])
                    cx.op("dve", lambda e: e.tensor_scalar(out=y[:, :], in0=z[:, :], scalar1=mv[:, 0:1], scalar2=mv[:, 4:5],
                                                           op0=ALU.subtract, op1=ALU.mult), reads=[bz, bsm], writes=[by])
                    cx.op("dve", lambda e: e.tensor_tensor(out=y[:, :], in0=y[:, :], in1=LNG[:, gi, :], op=ALU.mult),
                          reads=[by, B_LNG], writes=[by])
                    cx.op("pool", lambda e: e.tensor_tensor(out=y[:, :], in0=y[:, :], in1=LNG[:, gi + 1, :], op=ALU.add),
                          reads=[by, B_LNG], writes=[by])

                with ExitStack() as ph:
                    ph.enter_context(nc.named_scope('O%d' % l))
                    Wo = SB(ph, [128, KC, DM], BF16)
                    wst = [SB(ph, [128, KC, 256], F32) for _ in range(2)]
                    mx = [SB(ph, [128, DM], BF16) for _ in range(2)]
                    xr = [SB(ph, [128, DM], F32) for _ in range(2)]
                    mT = [SB(ph, [128, KC, 128], BF16) for _ in range(2)]
                    zt = [SB(ph, [128, DM], F32) for _ in range(2)]
                    yt = [SB(ph, [128, DM], F32) for _ in range(2)]
                    st6 = [SB(ph, [128, 12], F32) for _ in range(2)]
                    mv = [SB(ph, [128, 8], F32) for _ in range(2)]
                    PTB = [PS(ph, [128, 1024], BF16) for _ in range(2)]
                    POx = [[PS(ph, [128, 512]) for _ in range(2)] for _ in range(2)]
                    B_Wo = Buf()
                    B_wst, B_mx, B_xr, B_mT, B_zt, B_yt, B_sm, B_PTB = ([Buf(), Buf()] for _ in range(8))
                    B_POx = [[Buf(), Buf()], [Buf(), Buf()]]
                    cx.begin_fill(B_Wo)
                    for ci, c0 in enumerate(range(0, DM, 256)):
                        s = ci % 2
                        cx.dma("sp", wst[s][:, :, :], w_out_d[l, :, c0:c0 + 256].rearrange("(k p) c -> p k c", p=128), writes=[B_wst[s]])
                        copy_op(("act", "dve", "pool")[ci % 3], Wo[:, :, c0:c0 + 256], wst[s][:, :, :], reads=[B_wst[s]], pwrites=[B_Wo])
                    pend_o = [None]
                    for tt in range(T // 128):
                        s = tt % 2
                        r0 = tt * 128
                        cx.dma("sp", mx[s][:, :], MIX[r0:r0 + 128, :], reads=[cx.D("MIX", r0 // 512)], writes=[B_mx[s]])
                        cx.dma("sp", xr[s][:, :], x_src[r0:r0 + 128, :], reads=[cx.D(x_src_name, r0 // 512)], writes=[B_xr[s]])

                        def trm(e, s=s):
                            last = None
                            for kc in range(KC):
                                last = e.transpose(out=PTB[s][:, kc * 128:(kc + 1) * 128], in_=mx[s][:, kc * 128:(kc + 1) * 128],
                                                   identity=ident_b)
                            return last
                        cx.op("pe", trm, reads=[B_mx[s]] + CONSTS, writes=[B_PTB[s]])
                        copy_op("act", mT[s][:].rearrange("p k c -> p (k c)"), PTB[s][:, :], reads=[B_PTB[s]], writes=[B_mT[s]])
                        for hf in range(2):
                            def mo(e, s=s, hf=hf):
                                last = None
                                for kc in range(KC):
                                    last = e.matmul(POx[s][hf][:, :], lhsT=mT[s][:, kc, :], rhs=Wo[:, kc, hf * 512:(hf + 1) * 512],
                                                    start=(kc == 0), stop=(kc == KC - 1))
                                return last
                            cx.op("pe", mo, reads=[B_mT[s], B_Wo], writes=[B_POx[s][hf]])
                        cx.begin_fill(B_zt[s])
                        for hf in range(2):
                            cx.op("dve", lambda e, s=s, hf=hf: e.scalar_tensor_tensor(
                                out=zt[s][:, hf * 512:(hf + 1) * 512], in0=xr[s][:, hf * 512:(hf + 1) * 512], scalar=float(ALPHA),
                                in1=POx[s][hf][:, :], op0=ALU.mult, op1=ALU.add), reads=[B_xr[s], B_POx[s][hf]], pwrites=[B_zt[s]])
                        ln_part1(zt[s], B_zt[s], st6[s], mv[s], B_sm[s])

                        def fin(s=s, r0=r0):
                            ln_part2(zt[s], B_zt[s], yt[s], B_yt[s], 0, mv[s], B_sm[s])
                            cx.dma("pool", X1[r0:r0 + 128, :], yt[s][:, :], reads=[B_yt[s]], pwrites=[cx.D("X1", r0 // 512)])
                        if pend_o[0] is not None:
                            pend_o[0]()
                        pend_o[0] = fin
                    if pend_o[0] is not None:
                        pend_o[0]()
                    cx.barrier()

                with ExitStack() as ph:
                    ph.enter_context(nc.named_scope('F%d' % l))
                    Wd = SB(ph, [128, NFC, DM], BF16)
                    wst = [SB(ph, [128, 2, DM], F32) for _ in range(2)]
                    wg = [SB(ph, [128, KC, 256], BF16) for _ in range(3)]
                    xr = [SB(ph, [128, 4, DM], F32) for _ in range(2)]
                    hl = [SB(ph, [2, DM], F32) for _ in range(2)]
                    xT = SB(ph, [128, KC, 514], BF16)
                    hm = SB(ph, [128, NFC, 512], BF16)
                    Gt = [SB(ph, [128, 514], F32) for _ in range(2)]
                    cv = [SB(ph, [128, 512], F32) for _ in range(2)]
                    zt = [SB(ph, [128, DM], F32) for _ in range(2)]
                    yt = [SB(ph, [128, DM], F32) for _ in range(2)]
                    st6 = [SB(ph, [128, 12], F32) for _ in range(2)]
                    mv = [SB(ph, [128, 8], F32) for _ in range(2)]
                    PG = [PS(ph, [128, 512]) for _ in range(2)]
                    PUp = [PS(ph, [128, 512]) for _ in range(2)]
                    PH = PS(ph, [128, 512])
                    PD = [PS(ph, [128, 512]) for _ in range(2)]
                    PT = PS(ph, [128, 512])
                    B_Wd, B_xT, B_hm, B_PH, B_PT = Buf(), Buf(), Buf(), Buf(), Buf()
                    B_wst, B_xr, B_hl, B_Gt, B_cv, B_zt, B_yt, B_sm, B_PG, B_PUp, B_PD = ([Buf(), Buf()] for _ in range(11))
                    B_wg = [Buf(), Buf(), Buf()]
                    cx.begin_fill(B_Wd)
                    for ci in range(NFC // 2):
                        s = ci % 2
                        cx.dma("sp", wst[s][:, :, :], w_down_d[l, ci * 256:(ci + 1) * 256, :].rearrange("(c p) d -> p c d", p=128),
                               writes=[B_wst[s]])
                        copy_op(("act", "dve", "pool")[ci % 3], Wd[:, 2 * ci:2 * ci + 2, :], wst[s][:, :, :], reads=[B_wst[s]],
                                pwrites=[B_Wd])
                    gi_ = [0]
                    tiles = []
                    for (s_off, S) in seqs:
                        for t0 in range(s_off, s_off + S, 512):
                            tiles.append((t0, t0 == s_off, t0 + 512 == s_off + S))
                    wq = [(ti, fc) for ti in range(len(tiles)) for fc in range(NFC)]
                    wpos = [0]

                    def prefetch():
                        if wpos[0] < len(wq):
                            _, fc = wq[wpos[0]]
                            s3 = wpos[0] % 3
                            cx.dma("sp", wg[s3][:].rearrange("p k c -> p (k c)"), WGUd[l, fc], reads=[cx.D("WGU", (l, fc))],
                                   writes=[B_wg[s3]])
                            wpos[0] += 1
                    prefetch()
                    prefetch()
                    wi = 0
                    pend_f = [None]
                    for tix, (t0, first, last) in enumerate(tiles):
                        s = tix % 2
                        d1 = [cx.D("X1", t0 // 512)]
                        cx.dma("sp", xr[s][:], X1[t0:t0 + 512, :].rearrange("(a p) d -> p a d", p=128), reads=d1, writes=[B_xr[s]])
                        cx.begin_fill(B_hl[s])
                        if not first:
                            cx.dma("sp", hl[s][0:1, :], X1[t0 - 1:t0, :], reads=[cx.D("X1", (t0 - 1) // 512)], pwrites=[B_hl[s]])
                        if not last:
                            cx.dma("sp", hl[s][1:2, :], X1[t0 + 512:t0 + 513, :], reads=[cx.D("X1", (t0 + 512) // 512)], pwrites=[B_hl[s]])
                        cx.begin_fill(B_xT)
                        for kc in range(KC):
                            def tr(e, kc=kc, s=s):
                                last_ = None
                                for a in range(4):
                                    last_ = e.transpose(out=PT[:, a * 128:(a + 1) * 128], in_=xr[s][:, a, kc * 128:(kc + 1) * 128],
                                                        identity=ident_f)
                                return last_
                            cx.op("pe", tr, reads=[B_xr[s]] + CONSTS, writes=[B_PT])
                            copy_op(evac_eng(), xT[:, kc, 1:513], PT[:, :], reads=[B_PT], pwrites=[B_xT])
                        if first and last:
                            pass
                        if not (first and last):
                            def trh(e, s=s):
                                last_ = None
                                for kc in range(KC):
                                    last_ = e.transpose(out=PT[:, kc * 2:(kc + 1) * 2], in_=hl[s][0:2, kc * 128:(kc + 1) * 128],
                                                        identity=ident_f[0:2, 0:2])
                                return last_
                            cx.op("pe", trh, reads=[B_hl[s]] + CONSTS, writes=[B_PT])
                            cx.op("dve", lambda e: e.tensor_copy(out=xT[:, :, 0:514:513], in_=PT[:, 0:16].rearrange("p (k t) -> p k t", t=2)),
                                  reads=[B_PT], pwrites=[B_xT])
                        cx.begin_fill(B_hm)
                        for fc in range(NFC):
                            s3 = wi % 3
                            wi += 1
                            prefetch()
                            g2 = fc % 2
                            W = wg[s3]

                            def gm(e, W=W, g2=g2):
                                last_ = None
                                for kc in range(KC):
                                    last_ = e.matmul(PG[g2][:, :], lhsT=W[:, kc, 0:128], rhs=xT[:, kc, 1:513], start=(kc == 0), stop=(kc == KC - 1))
                                return last_
                            cx.op("pe", gm, reads=[B_wg[s3], B_xT], writes=[B_PG[g2]])
                            if not (first and last):
                                def gh(e, W=W, fc=fc):
                                    last_ = None
                                    for kc in range(KC):
                                        last_ = e.matmul(PH[:, fc * 2:fc * 2 + 2], lhsT=W[:, kc, 0:128], rhs=xT[:, kc, 0:514:513], start=(kc == 0),
                                                         stop=(kc == KC - 1), skip_group_check=True)
                                    return last_
                                cx.op("pe", gh, reads=[B_wg[s3], B_xT], writes=[B_PH])

                            def um(e, W=W, g2=g2):
                                last_ = None
                                for kc in range(KC):
                                    last_ = e.matmul(PUp[g2][:, :], lhsT=W[:, kc, 128:256], rhs=xT[:, kc, 1:513], start=(kc == 0), stop=(kc == KC - 1))
                                return last_
                            cx.op("pe", um, reads=[B_wg[s3], B_xT], writes=[B_PUp[g2]])
                            G = Gt[g2]
                            cx.begin_fill(B_Gt[g2])
                            cx.op("act", lambda e, G=G, g2=g2: e.copy(out=G[:, 1:513], in_=PG[g2][:, :]), reads=[B_PG[g2]], pwrites=[B_Gt[g2]])
                            if not (first and last):
                                cx.op("dve", lambda e, G=G, fc=fc: e.tensor_copy(out=G[:, 0:514:513], in_=PH[:, fc * 2:fc * 2 + 2]),
                                      reads=[B_PH], pwrites=[B_Gt[g2]])
                            if first:
                                cx.op("dve", lambda e, G=G: e.memset(G[:, 0:1], 0.0), reads=[B_Gt[g2]], pwrites=[B_Gt[g2]])
                            if last:
                                cx.op("dve", lambda e, G=G: e.memset(G[:, 513:514], 0.0), reads=[B_Gt[g2]], pwrites=[B_Gt[g2]])
                            C = cv[g2]
                            cx.op("dve", lambda e, G=G, C=C, fc=fc: e.tensor_scalar(out=C[:, :], in0=G[:, 1:513], scalar1=CW[:, fc, 1:2],
                                                                                  scalar2=CW[:, fc, 3:4], op0=ALU.mult, op1=ALU.add),
                                  reads=[B_Gt[g2], B_CW], writes=[B_cv[g2]])
                            cx.op("dve", lambda e, G=G, C=C, fc=fc: e.scalar_tensor_tensor(out=C[:, :], in0=G[:, 0:512], scalar=CW[:, fc, 0:1],
                                                                                         in1=C[:, :], op0=ALU.mult, op1=ALU.add),
                                  reads=[B_Gt[g2], B_CW, B_cv[g2]], writes=[B_cv[g2]])
                            cx.op("dve", lambda e, G=G, C=C, fc=fc: e.scalar_tensor_tensor(out=C[:, :], in0=G[:, 2:514], scalar=CW[:, fc, 2:3],
                                                                                         in1=C[:, :], op0=ALU.mult, op1=ALU.add),
                                  reads=[B_Gt[g2], B_CW, B_cv[g2]], writes=[B_cv[g2]])
                            cx.op("act", lambda e, C=C: e.activation(out=C[:, :], in_=C[:, :], func=AF.Gelu), reads=[B_cv[g2]], writes=[B_cv[g2]])
                            cx.op("dve", lambda e, C=C, fc=fc, g2=g2: e.tensor_tensor(out=hm[:, fc, :], in0=C[:, :], in1=PUp[g2][:, :], op=ALU.mult),
                                  reads=[B_cv[g2], B_PUp[g2]], pwrites=[B_hm])
                        for a in range(4):
                            z2 = a % 2
                            for hf in range(2):
                                def dm(e, a=a, hf=hf):
                                    last_ = None
                                    for fc in range(NFC):
                                        last_ = e.matmul(PD[hf][:, :], lhsT=hm[:, fc, a * 128:(a + 1) * 128], rhs=Wd[:, fc, hf * 512:(hf + 1) * 512],
                                                         start=(fc == 0), stop=(fc == NFC - 1))
                                    return last_
                                cx.op("pe", dm, reads=[B_hm, B_Wd], writes=[B_PD[hf]])
                            cx.begin_fill(B_zt[z2])
                            for hf in range(2):
                                cx.op("dve", lambda e, a=a, hf=hf, z2=z2, s=s: e.scalar_tensor_tensor(
                                    out=zt[z2][:, hf * 512:(hf + 1) * 512], in0=xr[s][:, a, hf * 512:(hf + 1) * 512], scalar=float(ALPHA),
                                    in1=PD[hf][:, :], op0=ALU.mult, op1=ALU.add), reads=[B_xr[s], B_PD[hf]], pwrites=[B_zt[z2]])
                            ln_part1(zt[z2], B_zt[z2], st6[z2], mv[z2], B_sm[z2])

                            def finf(z2=z2, r0=t0 + a * 128, t0=t0):
                                ln_part2(zt[z2], B_zt[z2], yt[z2], B_yt[z2], 2, mv[z2], B_sm[z2])
                                cx.dma("pool", y_dst[r0:r0 + 128, :], yt[z2][:, :], reads=[B_yt[z2]], pwrites=[cx.D(y_dst_name, t0 // 512)])
                            if pend_f[0] is not None:
                                pend_f[0]()
                            pend_f[0] = finf
                    if pend_f[0] is not None:
                        pend_f[0]()
                    cx.barrier()
            cx.barrier()
            for e in ("sp",):
                pass

        block.sync(main)
    return nc


SEQ_LENS = [4096, 4096, 2048, 2048, 2048, 2048]
_WNAMES = ["w_in", "gate_bias", "lam_q1", "lam_k1", "lam_q2", "lam_k2", "att_norm_g", "lstm_norm_g", "w_out",
           "ln1_g", "ln1_b", "w_gu", "conv_w", "conv_b", "w_down", "ln2_g", "ln2_b"]


def kernel(x_prompt, x_sample, **w):
    x_prompt = np.asarray(x_prompt, np.float32)
    x_sample = np.asarray(x_sample, np.float32)
    n = 8
    C, CBG = _build_consts()
    wd = {k: np.ascontiguousarray(np.asarray(w[k], np.float32)) for k in _WNAMES}
    in_maps = []
    for i in range(n):
        xs = np.concatenate([x_prompt[2 * i:2 * i + 2].reshape(-1, DM), x_sample[4 * i:4 * i + 4].reshape(-1, DM)], axis=0)
        m = dict(wd)
        m["x"] = np.ascontiguousarray(xs)
        m["consts"] = C
        m["constsb"] = CBG
        in_maps.append(m)
    nc = build(SEQ_LENS)
    res = run_bass_kernel_spmd(nc, in_maps, core_ids=list(range(n)))
    yp = np.empty_like(x_prompt)
    ys = np.empty_like(x_sample)
    for i in range(n):
        y = np.asarray(res.results[i]["y"], np.float32)
        yp[2 * i:2 * i + 2] = y[0:8192].reshape(2, 4096, DM)
        ys[4 * i:4 * i + 4] = y[8192:16384].reshape(4, 2048, DM)
    return (yp, ys)
```
